# Optimizing a Trainium2 kernel written in Bass

```python
import jax, jax.numpy as jnp
from jax import lax
import numpy as np

D_MODEL = 1024
BATCH = 16
SEQ = 2048
DEPTH = 1

GRID_W = 64
HEAD_DIM = 64
N_Q_HEADS = 8
N_KV_HEADS = 2
Q_PER_KV = N_Q_HEADS // N_KV_HEADS
ATTN_WIDTH = N_Q_HEADS * HEAD_DIM
KV_WIDTH = N_KV_HEADS * HEAD_DIM
LRU_WIDTH = D_MODEL - ATTN_WIDTH
LRU_BLOCKS = 8
LRU_BLOCK = LRU_WIDTH // LRU_BLOCKS
CONV_WIDTH = 4
LRU_C = 8.0
D_MIX = ATTN_WIDTH + LRU_WIDTH
D_IN = ATTN_WIDTH + 2 * KV_WIDTH + 2 * LRU_WIDTH
D_FF = -(-8 * D_MODEL // (3 * 256)) * 256
Q_BLOCK = 128
ROPE_THETA = 10000.0
EPS = 1e-6

kernel_name = "hymba_attn_rglru_hybrid_encoder"


def rms_norm(x, g):
    x32 = x.astype(jnp.float32)
    y = x32 * lax.rsqrt(jnp.mean(x32 * x32, axis=-1, keepdims=True) + EPS)
    return (y * g.astype(jnp.float32)).astype(x.dtype)


def axial_rope_tables(S):
    rows = S // GRID_W
    row = jnp.repeat(jnp.arange(rows, dtype=jnp.float32), GRID_W)
    col = jnp.tile(jnp.arange(GRID_W, dtype=jnp.float32), rows)
    axis_dim = HEAD_DIM // 2
    inv = ROPE_THETA ** (-jnp.arange(0, axis_dim, 2, dtype=jnp.float32) / axis_dim)
    ang_r = row[:, None] * inv[None, :]
    ang_c = col[:, None] * inv[None, :]
    ang = jnp.concatenate([ang_r, ang_r, ang_c, ang_c], axis=-1)
    return jnp.cos(ang), jnp.sin(ang)


def _rotate_half(z):
    z1, z2 = jnp.split(z, 2, axis=-1)
    return jnp.concatenate([-z2, z1], axis=-1)


def apply_axial_rope(x, cos, sin):
    x32 = x.astype(jnp.float32)
    xr, xc = jnp.split(x32, 2, axis=-1)
    rot = jnp.concatenate([_rotate_half(xr), _rotate_half(xc)], axis=-1)
    out = x32 * cos[None, :, None, :] + rot * sin[None, :, None, :]
    return out.astype(x.dtype)


def block_attention(q, k, v):
    B, S = q.shape[0], q.shape[1]
    nblk = S // Q_BLOCK
    qb = q.reshape(B, nblk, Q_BLOCK, N_KV_HEADS, Q_PER_KV, HEAD_DIM).transpose(1, 0, 3, 4, 2, 5)
    kt = k.transpose(0, 2, 1, 3)
    vt = v.transpose(0, 2, 1, 3)
    scale = HEAD_DIM ** -0.5

    def one_block(qblk):
        s = jnp.einsum('bkgqd,bksd->bkgqs', qblk, kt,
                       preferred_element_type=jnp.float32) * scale
        p = jax.nn.softmax(s, axis=-1).astype(vt.dtype)
        return jnp.einsum('bkgqs,bksd->bkgqd', p, vt)

    ob = lax.map(one_block, qb)
    return ob.transpose(1, 0, 4, 2, 3, 5).reshape(B, S, ATTN_WIDTH)


def centred_depthwise_conv(x, w, b):
    S = x.shape[1]
    left = (CONV_WIDTH - 1) // 2
    xp = jnp.pad(x, ((0, 0), (left, CONV_WIDTH - 1 - left), (0, 0)))
    out = b[None, None, :]
    for j in range(CONV_WIDTH):
        out = out + xp[:, j:j + S, :] * w[j][None, None, :]
    return out


def _linear_combine(c1, c2):
    a1, b1 = c1
    a2, b2 = c2
    return a1 * a2, a2 * b1 + b2


def rg_lru_direction(x, w_r, b_r, w_i, b_i, lam, reverse):
    B, S, W = x.shape
    xb = x.reshape(B, S, LRU_BLOCKS, LRU_BLOCK)
    r = jax.nn.sigmoid(jnp.einsum('bsnc,ncd->bsnd', xb, w_r.astype(jnp.float32)).reshape(B, S, W)
                       + b_r.astype(jnp.float32))
    i = jax.nn.sigmoid(jnp.einsum('bsnc,ncd->bsnd', xb, w_i.astype(jnp.float32)).reshape(B, S, W)
                       + b_i.astype(jnp.float32))
    log_a = -LRU_C * r * jax.nn.softplus(-lam.astype(jnp.float32))
    a = jnp.exp(log_a)
    inp = jnp.sqrt(-jnp.expm1(2.0 * log_a)) * (i * x)
    _, h = lax.associative_scan(_linear_combine, (a, inp), reverse=reverse, axis=1)
    return h


def setup_inputs(seed: int = 0) -> dict:
    key = jax.random.key(seed)
    ks = jax.random.split(key, 24)
    f32 = jnp.float32

    def nrm(k, shape, fan_in, extra=1.0):
        return jax.random.normal(k, shape, f32) * (fan_in ** -0.5) * extra

    def gain(k, shape):
        return 1.0 + 0.02 * jax.random.normal(k, shape, f32)

    def bias(k, shape):
        return 0.02 * jax.random.normal(k, shape, f32)

    res_scale = (2.0 * DEPTH) ** -0.5
    x = jax.random.normal(ks[0], (BATCH, SEQ, D_MODEL), f32)
    norm_mix = gain(ks[1], (DEPTH, D_MODEL))
    w_in = nrm(ks[2], (DEPTH, D_MODEL, D_IN), D_MODEL)
    q_norm = gain(ks[3], (DEPTH, HEAD_DIM))
    k_norm = gain(ks[4], (DEPTH, HEAD_DIM))
    conv_w = nrm(ks[5], (DEPTH, CONV_WIDTH, LRU_WIDTH), CONV_WIDTH)
    conv_b = bias(ks[6], (DEPTH, LRU_WIDTH))
    w_rgate = nrm(ks[7], (DEPTH, 2, LRU_BLOCKS, LRU_BLOCK, LRU_BLOCK), LRU_BLOCK)
    b_rgate = bias(ks[8], (DEPTH, 2, LRU_WIDTH))
    w_igate = nrm(ks[9], (DEPTH, 2, LRU_BLOCKS, LRU_BLOCK, LRU_BLOCK), LRU_BLOCK)
    b_igate = bias(ks[10], (DEPTH, 2, LRU_WIDTH))
    a_pow_c = jax.random.uniform(ks[11], (DEPTH, 2, LRU_WIDTH), f32, 0.9, 0.999)
    a_base = a_pow_c ** (1.0 / LRU_C)
    lru_lambda = jnp.log(a_base) - jnp.log1p(-a_base)
    out_norm_attn = gain(ks[12], (DEPTH, ATTN_WIDTH))
    out_norm_lru = gain(ks[13], (DEPTH, LRU_WIDTH))
    w_out = nrm(ks[14], (DEPTH, D_MIX, D_MODEL), D_MIX, res_scale)
    norm_ffn = gain(ks[15], (DEPTH, D_MODEL))
    w_gate = nrm(ks[16], (DEPTH, D_MODEL, D_FF), D_MODEL)
    w_up = nrm(ks[17], (DEPTH, D_MODEL, D_FF), D_MODEL)
    w_down = nrm(ks[18], (DEPTH, D_FF, D_MODEL), D_FF, res_scale)
    return {"x": x, "norm_mix": norm_mix, "w_in": w_in, "q_norm": q_norm, "k_norm": k_norm,
            "conv_w": conv_w, "conv_b": conv_b, "w_rgate": w_rgate, "b_rgate": b_rgate,
            "w_igate": w_igate, "b_igate": b_igate, "lru_lambda": lru_lambda,
            "out_norm_attn": out_norm_attn, "out_norm_lru": out_norm_lru, "w_out": w_out,
            "norm_ffn": norm_ffn, "w_gate": w_gate, "w_up": w_up, "w_down": w_down}


def reference(x, norm_mix, w_in, q_norm, k_norm, conv_w, conv_b, w_rgate, b_rgate,
              w_igate, b_igate, lru_lambda, out_norm_attn, out_norm_lru, w_out,
              norm_ffn, w_gate, w_up, w_down):
    B, S, _ = x.shape
    cos, sin = axial_rope_tables(S)
    splits = [ATTN_WIDTH, ATTN_WIDTH + KV_WIDTH, ATTN_WIDTH + 2 * KV_WIDTH,
              ATTN_WIDTH + 2 * KV_WIDTH + LRU_WIDTH]
    h = x
    for l in range(DEPTH):
        u = rms_norm(h, norm_mix[l])
        proj = jnp.einsum('bsd,de->bse', u, w_in[l])
        q, k, v, xl, gl = jnp.split(proj, splits, axis=-1)
        q = q.reshape(B, S, N_Q_HEADS, HEAD_DIM)
        k = k.reshape(B, S, N_KV_HEADS, HEAD_DIM)
        v = v.reshape(B, S, N_KV_HEADS, HEAD_DIM)
        q = apply_axial_rope(rms_norm(q, q_norm[l]), cos, sin)
        k = apply_axial_rope(rms_norm(k, k_norm[l]), cos, sin)
        attn_out = block_attention(q, k, v)

        xc = centred_depthwise_conv(xl, conv_w[l], conv_b[l]).astype(jnp.float32)
        y_fwd = rg_lru_direction(xc, w_rgate[l, 0], b_rgate[l, 0], w_igate[l, 0],
                                 b_igate[l, 0], lru_lambda[l, 0], reverse=False)
        y_bwd = rg_lru_direction(xc, w_rgate[l, 1], b_rgate[l, 1], w_igate[l, 1],
                                 b_igate[l, 1], lru_lambda[l, 1], reverse=True)
        lru_out = ((y_fwd + y_bwd) * jax.nn.gelu(gl.astype(jnp.float32))).astype(h.dtype)

        mixed = jnp.concatenate([rms_norm(attn_out, out_norm_attn[l]),
                                 rms_norm(lru_out, out_norm_lru[l])], axis=-1)
        h = h + jnp.einsum('bse,ed->bsd', mixed, w_out[l])

        u = rms_norm(h, norm_ffn[l])
        ff = jax.nn.silu(jnp.einsum('bsd,df->bsf', u, w_gate[l])) * jnp.einsum('bsd,df->bsf', u, w_up[l])
        h = h + jnp.einsum('bsf,fd->bsd', ff, w_down[l])
    return h
```

```python
from contextlib import ExitStack

import numpy as np
import concourse.bass as bass
import concourse.mybir as mybir
from concourse.bass_utils import run_bass_kernel_spmd

F32 = mybir.dt.float32
BF16 = mybir.dt.bfloat16
ALU = mybir.AluOpType
AF = mybir.ActivationFunctionType
COMPUTE = ("pe", "act", "dve", "pool")

N_CORES = 8
S = 2048
NSEQ = 2
NT = NSEQ * S
D = 1024
DFF = 2816
NFC = DFF // 128
EPS = 1e-6


class Prog:
    def __init__(self, nc, same_engine_sync=("act", "dve", "pool")):
        self.nc = nc
        self.ops = []
        self.last_writer = {}
        self.readers = {}
        self.same_engine_sync = set(same_engine_sync)
        self.dma_groups = {}
        self.epoch_op = None
        self.last_on_eng = {}

    def op(self, eng, fn, reads=(), writes=(), dma=None, extra_deps=()):
        deps = set(extra_deps)
        pr = [k for k in reads if isinstance(k, str) and k.startswith("ps")]
        if pr:
            reads = [k for k in reads if k not in pr]
            writes = list(writes) + [k for k in pr if k not in writes]
        for k in reads:
            w = self.last_writer.get(k)
            if w is not None:
                deps.add(w)
        for k in writes:
            w = self.last_writer.get(k)
            if w is not None:
                deps.add(w)
            deps.update(self.readers.get(k, ()))
        if self.epoch_op is not None:
            deps.add(self.epoch_op)
        red = {}
        for d_ in deps:
            do = self.ops[d_]
            kk = do["eng"] if do["dma"] is None else ("dma", do["dma"])
            if d_ > red.get(kk, -1):
                red[kk] = d_
        deps = set(red.values())
        oid = len(self.ops)
        self.ops.append(dict(eng=eng, fn=fn, deps=deps, dma=dma))
        if dma is not None:
            self.dma_groups.setdefault(dma, []).append(oid)
            self.last_on_eng[("dma", dma)] = oid
        else:
            self.last_on_eng[eng] = oid
        for k in reads:
            self.readers.setdefault(k, []).append(oid)
        for k in writes:
            self.last_writer[k] = oid
            self.readers[k] = []
        return oid

    def barrier(self, scratch):
        deps = set(self.last_on_eng.values())
        self.epoch_op = None
        oid = self.op("dve", lambda e: e.memset(scratch, 0.0), extra_deps=deps)
        self.epoch_op = oid
        return oid

    def emit(self, stack, final_wait_engine="sp"):
        nc = self.nc
        ops = self.ops
        for o in ops:
            nd = set()
            for d in o["deps"]:
                do = ops[d]
                if do["dma"] is None and o["dma"] is None and do["eng"] == o["eng"]:
                    if o["eng"] not in self.same_engine_sync:
                        continue
                nd.add(d)
            o["deps"] = nd
        marked = set()
        for o in ops:
            for d in o["deps"]:
                if ops[d]["dma"] is None:
                    marked.add(d)
        esem = {e: stack.enter_context(nc.semaphore("s_" + e)) for e in COMPUTE}
        gsem = {g: stack.enter_context(nc.semaphore("g_" + str(g))) for g in self.dma_groups}
        cnt = {e: 0 for e in COMPUTE}
        for i, o in enumerate(ops):
            if o["dma"] is None and i in marked:
                cnt[o["eng"]] += 1
                o["val"] = cnt[o["eng"]]
        import bisect

        def dep_wait(dep_id, user_id):
            do = ops[dep_id]
            if do["dma"] is None:
                return esem[do["eng"]], do["val"]
            g = self.dma_groups[do["dma"]]
            n = bisect.bisect_left(g, user_id)
            return gsem[do["dma"]], 16 * n

        per_eng = {}
        for i, o in enumerate(ops):
            per_eng.setdefault(o["eng"], []).append(i)
        final = [(gsem[g], 16 * len(lst)) for g, lst in self.dma_groups.items()]
        block = stack.enter_context(nc.Block())
        reg = {"pe": block.tensor, "act": block.scalar, "dve": block.vector,
               "pool": block.gpsimd, "sp": block.sync}
        if final_wait_engine not in per_eng:
            per_eng[final_wait_engine] = []
        stats = dict(waits=0)

        def run_engine(e, lst, engine):
            waited = {}
            for i in lst:
                o = ops[i]
                need = {}
                for d in o["deps"]:
                    s, v = dep_wait(d, i)
                    if v > need.get(s.num, (None, 0))[1]:
                        need[s.num] = (s, v)
                todo = [(s, v) for key, (s, v) in need.items() if waited.get(key, 0) < v]
                embed = None
                if todo and e != "pe":
                    embed = todo.pop()
                for s, v in todo:
                    engine.wait_ge(s, v)
                    waited[s.num] = v
                    stats["waits"] += 1
                ins = o["fn"](engine)
                if embed is not None:
                    ins._wait_ge(embed[0], embed[1])
                    waited[embed[0].num] = embed[1]
                if o["dma"] is not None:
                    ins.then_inc(gsem[o["dma"]], 16)
                elif i in marked:
                    ins.then_inc(esem[e], 1)
            if e == final_wait_engine:
                for s, v in final:
                    if waited.get(s.num, 0) < v:
                        engine.wait_ge(s, v)

        for e, lst in per_eng.items():
            def f(engine, e=e, lst=lst):
                run_engine(e, lst, engine)
            reg[e](f)
        self.stats = dict(n_ops=len(ops), n_waits=stats["waits"], marked=len(marked),
                          per_eng={e: len(l) for e, l in per_eng.items()})


class Region:
    def __init__(self, arena, start, end):
        self.arena, self.start, self.end, self.ptr = arena, start, end, start

    def reset(self):
        self.ptr = self.start

    def alloc(self, shape, dt):
        n = 1
        for s in shape[1:]:
            n *= s
        nbytes = n * (4 if dt == F32 else 2)
        nbytes_al = (nbytes + 31) // 32 * 32
        off = self.ptr
        assert off + nbytes_al <= self.end, (off, nbytes_al, self.end)
        self.ptr += nbytes_al
        v = self.arena[:, off // 2:(off + nbytes) // 2]
        if dt == F32:
            v = v.bitcast(F32)
        if len(shape) == 3:
            v = v.rearrange("p (a b) -> p a b", a=shape[1])
        return v


def build(debug=False):
    nc = bass.Bass("TRN2", target_bir_lowering=False)

    def din(name, shape):
        return nc.dram_tensor(name, list(shape), F32, kind="ExternalInput").ap()

    xT = din("xT", [D, NT])
    w_in = din("w_in", [D, 1792])
    w_out = din("w_out", [D, D])
    w_gate = din("w_gate", [D, DFF])
    w_up = din("w_up", [D, DFF])
    w_down = din("w_down", [DFF, D])
    w_rg = din("w_rg", [2, 8, 64, 64])
    w_ig = din("w_ig", [2, 8, 64, 64])
    prm = din("prm", [128, 72])
    cmat = din("cmat", [128, 4, 128])
    rope = din("rope", [128, 2, S])
    outT = nc.dram_tensor("outT", [D, NT], F32, kind="ExternalOutput").ap()

    xT3 = xT.rearrange("(k p) t -> p k t", p=128)
    outT3 = outT.rearrange("(k p) t -> p k t", p=128)

    with ExitStack() as st:
        ARENA_BYTES = 212480
        arena = st.enter_context(nc.sbuf_tensor("arena", [128, ARENA_BYTES // 2], BF16))
        ps_all = st.enter_context(nc.psum_tensor("ps_all", [128, 4096], F32))[:]
        psb = [ps_all[:, i * 512:(i + 1) * 512] for i in range(8)]
        PSK = ["ps%d" % i for i in range(8)]

        R_CONST = Region(arena, 0, 10240)
        R_OUTS = Region(arena, 10240, 59392)
        R_QKV = Region(arena, 59392, 86528)
        R_XG = Region(arena, 86528, 135808)
        R_TMP = Region(arena, 135808, ARENA_BYTES)
        R_D = Region(arena, 59392, ARENA_BYTES)

        P = Prog(nc)

        PR = R_CONST.alloc([128, 72], F32)
        DV = R_CONST.alloc([128, 40], F32)
        cm_bf = R_CONST.alloc([128, 4, 128], BF16)
        Wg = R_CONST.alloc([128, 16, 128], BF16)
        scr = R_CONST.alloc([128, 8], F32)
        Rrow = R_CONST.alloc([128, 512], F32)
        Rhl = R_CONST.alloc([128, 2, 512], BF16)
        ones_bf = cm_bf[:, 0, :]
        blk_bf = cm_bf[:, 1, :]
        perm_bf = cm_bf[:, 2, :]
        sel_bf = cm_bf[:, 3, :]
        C_GMIX, C_GFFN, C_GQ, C_GK, C_GATT, C_GLRU = 0, 8, 16, 17, 18, 22
        C_CW, C_CB, C_BR, C_BI, C_LAM, C_EPS, C_ONE = 26, 42, 46, 54, 62, 70, 71
        V_HBR, V_HBI, V_C1, V_C2, V_Q = 0, 8, 16, 24, 32

        def col(t, c):
            return t[:, c:c + 1]

        P.op("sp", lambda e: e.dma_start(out=PR, in_=prm), writes=["PR"], dma="c_prm")
        P.op("pool", lambda e: e.dma_start(out=cm_bf, in_=cmat), writes=["cm"], dma="c_cm")
        P.op("dve", lambda e: e.memset(Wg.rearrange("p a b -> p (a b)"), 0.0), writes=["Wg"])
        P.op("dve", lambda e: e.memset(Rrow, 0.0), writes=["Rrow"])
        P.op("dve", lambda e: e.memset(Rhl.rearrange("p a b -> p (a b)"), 0.0), writes=["Rhl"])
        for dr in range(2):
            for gi, wsrc in enumerate((w_rg, w_ig)):
                wv = wsrc[dr].rearrange("(cc h) c d -> h c cc d", h=2)
                for h in range(2):
                    idx = dr * 8 + gi * 4
                    P.op("pool", lambda e, wv=wv, h=h, idx=idx: e.dma_start(
                        out=Wg[h * 64:(h + 1) * 64, idx:idx + 4, h * 64:(h + 1) * 64], in_=wv[h]),
                        reads=[], writes=["Wg"], dma="c_wg")
        P.op("dve", lambda e: e.tensor_scalar(out=DV[:, V_HBR:V_HBR + 16], in0=PR[:, C_BR:C_BR + 16],
                                              scalar1=0.5, scalar2=None, op0=ALU.mult),
             reads=["PR"], writes=["DVb"])
        P.op("act", lambda e: e.activation(out=DV[:, V_C1:V_C1 + 8], in_=PR[:, C_LAM:C_LAM + 8],
                                           func=AF.Exp, scale=-1.0), reads=["PR"], writes=["DVc"])
        P.op("act", lambda e: e.activation(out=DV[:, V_C1:V_C1 + 8], in_=DV[:, V_C1:V_C1 + 8],
                                           func=AF.Ln, bias=col(PR, C_ONE), scale=1.0),
             reads=["PR", "DVc"], writes=["DVc"])
        P.op("dve", lambda e: e.tensor_scalar(out=DV[:, V_C2:V_C2 + 8], in0=DV[:, V_C1:V_C1 + 8],
                                              scalar1=-8.0, scalar2=None, op0=ALU.mult),
             reads=["DVc"], writes=["DVc2"])
        P.op("dve", lambda e: e.tensor_scalar(out=DV[:, V_C1:V_C1 + 8], in0=DV[:, V_C1:V_C1 + 8],
                                              scalar1=-4.0, scalar2=None, op0=ALU.mult),
             reads=["DVc"], writes=["DVc"])
        P.op("dve", lambda e: e.memset(DV[:, V_Q:V_Q + 1], 0.25), writes=["DVq"])

        attnT = R_OUTS.alloc([128, 4, S], BF16)
        lruT = R_OUTS.alloc([128, 4, S], BF16)
        r_a = R_OUTS.alloc([128, S], F32)
        r_l = R_OUTS.alloc([128, S], F32)
        outs_end = R_OUTS.ptr
        R_OUTS_A = Region(arena, 10240, 59392)

        qT = R_QKV.alloc([128, 4, S], BF16)
        kT = R_QKV.alloc([128, S], BF16)
        V0 = R_QKV.alloc([128, 16, 65], BF16)
        V1 = R_QKV.alloc([128, 16, 128], BF16)
        xlp = R_XG.alloc([128, 4, S + 4], F32)
        gg = R_XG.alloc([128, 4, S], BF16)

        rr = {"ps": 0}

        def psum():
            i = rr["ps"] % 8
            rr["ps"] += 1
            return psb[i], PSK[i]

        def rstd_from_psum(bank, bk, scale, out_ap, out_key):
            P.op("act", lambda e: e.activation(out=out_ap, in_=bank, func=AF.Ln,
                                               bias=col(PR, C_EPS), scale=scale),
                 reads=[bk, "PR"], writes=[out_key])
            P.op("act", lambda e: e.activation(out=out_ap, in_=out_ap, func=AF.Exp, scale=-0.5),
                 reads=[out_key], writes=[out_key])

        warm_rhs = Wg.rearrange("p a b -> p (a b)")[:, 0:512]

        def warm(n=24):
            for _ in range(n):
                P.op("pe", lambda e: e.matmul(psb[7], lhsT=ones_bf, rhs=warm_rhs, start=True, stop=True),
                     reads=["cm", "Wg"], writes=[PSK[7]])

        for sq in range(NSEQ):
            tb = sq * S
            P.barrier(col(scr, 0))
            warm()
            R_TMP.reset()
            RA = R_OUTS_A
            RA.reset()
            w_in_bf = R_TMP.alloc([128, 8, 1792], BF16)
            ropet = R_TMP.alloc([128, 2, S], F32)
            xin = R_TMP.alloc([128, 8, 512], F32)
            uTb = [R_TMP.alloc([128, 8, 512], BF16), RA.alloc([128, 8, 512], BF16)]
            xsq = [R_TMP.alloc([128, 512], BF16) for _ in range(6)] + [RA.alloc([128, 512], BF16) for _ in range(2)]
            rb = RA.alloc([128, 512], F32)
            NB3 = 3
            sqb = [RA.alloc([128, 512], BF16) for _ in range(NB3)]
            rsb = [RA.alloc([128, 512], F32) for _ in range(NB3)]
            qnb = [RA.alloc([128, 512], BF16) for _ in range(NB3)]
            t1b = [RA.alloc([128, 512], F32) for _ in range(NB3)]
            t2b = [RA.alloc([128, 512], F32) for _ in range(NB3)]
            z2b = [RA.alloc([128, 512], F32) for _ in range(2)]
            zhb = [RA.alloc([128, 512], F32) for _ in range(2)]
            thb = [RA.alloc([128, 512], F32) for _ in range(2)]

            w_in3 = w_in.rearrange("(k p) n -> p k n", p=128)
            for wi, (c0_, c1_) in enumerate(((0, 640), (640, 768), (768, 1280), (1280, 1792))):
                P.op("pool", lambda e, c0_=c0_, c1_=c1_: e.dma_start(out=w_in_bf[:, :, c0_:c1_], in_=w_in3[:, :, c0_:c1_]),
                     writes=["w_in%d" % wi], dma="w_in%d" % wi)
            P.op("pool", lambda e: e.memset(V1.rearrange("p a b -> p (a b)"), 0.0), writes=["V1c"])
            P.op("pool", lambda e: e.memset(V1[:, :, 0:1], 1.0), writes=["V1c"])
            P.op("pool", lambda e: e.memset(V0[:, :, 64:65], 1.0), writes=["V0c"])
            P.op("pool", lambda e: e.memset(xlp[:, :, 0:1], 0.0), writes=["xlpad"])
            P.op("pool", lambda e: e.memset(xlp[:, :, S + 1:S + 4], 0.0), writes=["xlpad"])

            def pro1(tc, ks=range(8), dma=True):
                g0 = tb + tc * 512
                if dma:
                    P.op("sp", lambda e: e.dma_start(out=xin, in_=xT3[:, :, g0:g0 + 512]), writes=["xin"], dma="xin")
                for k in ks:
                    P.op("act", lambda e, k=k: e.activation(out=xsq[k], in_=xin[:, k, :], func=AF.Square),
                         reads=["xin"], writes=["xsq%d" % k])

            pro_state = {}

            def pro2(tc):
                ssb, ssk = psum()
                pro_state[tc] = (ssb, ssk)
                for k in range(8):
                    P.op("pe", lambda e, k=k: e.matmul(ssb, lhsT=ones_bf, rhs=xsq[k], start=(k == 0), stop=(k == 7)),
                         reads=["cm", "xsq%d" % k], writes=[ssk])
                rstd_from_psum(ssb, ssk, 1.0 / D, rb, "rb")

            def pro3(tc, ks=range(8)):
                uT = uTb[tc % 2]
                for k in ks:
                    P.op("dve", lambda e, k=k: e.scalar_tensor_tensor(
                        out=uT[:, k, :], in0=xin[:, k, :], scalar=col(PR, C_GMIX + k), in1=rb,
                        op0=ALU.mult, op1=ALU.mult),
                        reads=["xin", "rb", "PR"], writes=[("uT", tc % 2, k)])

            cnt = {"i": 0, "g": 0}

            def make_tasks(tc):
                t0 = tc * 512
                uT = uTb[tc % 2]
                UTK = [("uT", tc % 2, k) for k in range(8)]
                tasks = []

                def proj(fc0, bank, bk):
                    WINK = ["w_in%d" % (0 if fc0 < 640 else (1 if fc0 < 768 else (2 if fc0 < 1280 else 3)))]
                    for k in range(8):
                        P.op("pe", lambda e, k=k: e.matmul(
                            bank, lhsT=w_in_bf[:, k, fc0:fc0 + 128], rhs=uT[:, k, :],
                            start=(k == 0), stop=(k == 7)),
                            reads=UTK + WINK, writes=[bk])

                def qk_task(fc):
                    i = cnt["i"]; cnt["i"] += 1
                    b3 = i % NB3
                    stt = {}

                    def s1():
                        X, xk = psum()
                        stt["X"] = (X, xk)
                        proj(fc * 128, X, xk)
                        P.op("act", lambda e: e.activation(out=sqb[b3], in_=X, func=AF.Square),
                             reads=[xk], writes=["sqb%d" % b3])

                    def s2():
                        X, xk = stt["X"]
                        Y, yk = psum()
                        P.op("pe", lambda e: e.matmul(Y, lhsT=blk_bf, rhs=sqb[b3], start=True, stop=True),
                             reads=["cm", "sqb%d" % b3], writes=[yk])
                        rstd_from_psum(Y, yk, 1.0, rsb[b3], "rsb%d" % b3)
                        gcol = C_GQ if fc < 4 else C_GK
                        P.op("dve", lambda e: e.scalar_tensor_tensor(
                            out=qnb[b3], in0=X, scalar=col(PR, gcol), in1=rsb[b3], op0=ALU.mult, op1=ALU.mult),
                            reads=[xk, "rsb%d" % b3, "PR"], writes=["qnb%d" % b3])

                    def s3():
                        Z, zk = psum()
                        P.op("pe", lambda e: e.matmul(Z, lhsT=perm_bf, rhs=qnb[b3], start=True, stop=True),
                             reads=["cm", "qnb%d" % b3], writes=[zk])
                        P.op("dve", lambda e: e.tensor_tensor(
                            out=t1b[b3], in0=qnb[b3], in1=ropet[:, 0, t0:t0 + 512], op=ALU.mult),
                            reads=["qnb%d" % b3, "rope"], writes=["t1b%d" % b3])
                        P.op("dve", lambda e: e.tensor_tensor(
                            out=t2b[b3], in0=Z, in1=ropet[:, 1, t0:t0 + 512], op=ALU.mult),
                            reads=[zk, "rope"], writes=["t2b%d" % b3])
                        dst = qT[:, fc, t0:t0 + 512] if fc < 4 else kT[:, t0:t0 + 512]
                        P.op("pool", lambda e: e.tensor_tensor(out=dst, in0=t1b[b3], in1=t2b[b3], op=ALU.add),
                             reads=["t1b%d" % b3, "t2b%d" % b3], writes=[("qk", fc, tc)])
                    return (s1, s2, s3)

                def v_task():
                    def s1():
                        VB, vk = psum()
                        for j in range(4):
                            for k in range(8):
                                P.op("pe", lambda e, j=j, k=k: e.matmul(
                                    VB[:, j * 128:(j + 1) * 128], lhsT=uT[:, k, j * 128:(j + 1) * 128],
                                    rhs=w_in_bf[:, k, 640:768], start=(k == 0), stop=(k == 7)),
                                    reads=UTK + ["w_in1"], writes=[vk])
                        VB3 = VB.rearrange("p (a b) -> p a b", a=4)
                        P.op("act", lambda e: e.activation(
                            out=V0[:, tc * 4:tc * 4 + 4, 0:64], in_=VB3[:, :, 0:64], func=AF.Copy),
                            reads=[vk, "V0c"], writes=[("V0", tc)])
                        P.op("dve", lambda e: e.tensor_copy(
                            out=V1[:, tc * 4:tc * 4 + 4, 64:128], in_=VB3[:, :, 64:128]),
                            reads=[vk, "V1c"], writes=[("V1", tc)])
                    return (s1, None, None)

                def xl_task(cc):
                    def s1():
                        X, xk = psum()
                        proj(768 + cc * 128, X, xk)
                        P.op("act", lambda e: e.activation(
                            out=xlp[:, cc, 1 + t0:1 + t0 + 512], in_=X, func=AF.Copy),
                            reads=[xk, "xlpad"], writes=[("xl", cc, tc)])
                    return (s1, None, None)

                def gl_task(cc):
                    i = cnt["g"]; cnt["g"] += 1
                    b2 = i % 2

                    def s1():
                        X, xk = psum()
                        proj(1280 + cc * 128, X, xk)
                        P.op("act", lambda e: e.activation(out=z2b[b2], in_=X, func=AF.Square),
                             reads=[xk], writes=["z2b%d" % b2])
                        P.op("act", lambda e: e.activation(out=zhb[b2], in_=X, func=AF.Copy, scale=0.5),
                             reads=[xk], writes=["zhb%d" % b2])
                        P.op("dve", lambda e: e.tensor_scalar(
                            out=z2b[b2], in0=z2b[b2], scalar1=0.044715, scalar2=1.0, op0=ALU.mult, op1=ALU.add),
                            reads=["z2b%d" % b2], writes=["z2b%d" % b2])
                        P.op("dve", lambda e: e.tensor_tensor(
                            out=z2b[b2], in0=z2b[b2], in1=zhb[b2], op=ALU.mult),
                            reads=["z2b%d" % b2, "zhb%d" % b2], writes=["z2b%d" % b2])

                    def s2():
                        P.op("act", lambda e: e.activation(
                            out=thb[b2], in_=z2b[b2], func=AF.Tanh, scale=2.0 * 0.7978845608028654),
                            reads=["z2b%d" % b2], writes=["thb%d" % b2])
                        P.op("dve", lambda e: e.scalar_tensor_tensor(
                            out=gg[:, cc, t0:t0 + 512], in0=thb[b2], scalar=1.0, in1=zhb[b2],
                            op0=ALU.add, op1=ALU.mult),
                            reads=["thb%d" % b2, "zhb%d" % b2], writes=[("gg", cc, tc)])
                    return (s1, s2, None)

                for fc in range(5):
                    tasks.append(qk_task(fc))
                tasks.append(v_task())
                for cc in range(4):
                    tasks.append(xl_task(cc))
                for cc in range(4):
                    tasks.append(gl_task(cc))
                return tasks

            pro1(0)
            P.op("sp", lambda e: e.dma_start(out=ropet, in_=rope), writes=["rope"], dma="rope")
            pro2(0)
            pro3(0)
            for tc in range(4):
                tasks = make_tasks(tc)
                n = len(tasks)
                for i in range(n + 3):
                    if i < n:
                        tasks[i][0]()
                    if 0 <= i - 1 < n and tasks[i - 1][1] is not None:
                        tasks[i - 1][1]()
                    if 0 <= i - 3 < n and tasks[i - 3][2] is not None:
                        tasks[i - 3][2]()
                    if tc + 1 < 4:
                        if 1 <= i <= 4:
                            pro1(tc + 1, ks=(2 * (i - 1), 2 * (i - 1) + 1), dma=(i == 1))
                        if i == 6:
                            pro2(tc + 1)
                        if 8 <= i <= 11:
                            pro3(tc + 1, ks=(2 * (i - 8), 2 * (i - 8) + 1))

            P.barrier(col(scr, 1))
            warm()
            R_TMP.reset()
            xc = R_TMP.alloc([128, 4, S], F32)
            Pb = [R_TMP.alloc([128, 1024], BF16) for _ in range(3)]
            osb = [R_TMP.alloc([128, 512], F32) for _ in range(2)]
            an = [R_TMP.alloc([128, 512], F32) for _ in range(2)]
            asq = [R_TMP.alloc([128, 512], BF16) for _ in range(2)]
            dn = R_TMP.alloc([128, 512], F32)
            P.op("dve", lambda e: e.memset(dn, 1.0), writes=["dn"])
            QKALL = [("qk", fc, t) for fc in range(5) for t in range(4)]
            VALL = [("V0", t) for t in range(4)] + [("V1", t) for t in range(4)] + ["V0c", "V1c"]
            S2 = [ps_all[:, 0:1024], ps_all[:, 1024:2048]]
            S2K = [[PSK[0], PSK[1]], [PSK[2], PSK[3]]]
            OA, oak = psb[4], PSK[4]
            OB, obk = psb[5], PSK[5]
            RBp, rbk = psb[6], PSK[6]
            NA, nak = psb[7], PSK[7]
            steps = [(qc, j, stl) for qc in range(4) for j in range(4) for stl in range(16)]
            nst = len(steps)
            pending = []

            def emit_qk(si):
                qc, j, stl = steps[si]
                q0 = qc * 512
                sl = si % 2
                for u in range(2):
                    rows = slice(0, 64) if u == 0 else slice(64, 128)
                    P.op("pe", lambda e, u=u, rows=rows: e.matmul(
                        S2[sl][:, u * 512:(u + 1) * 512], lhsT=kT[rows, stl * 128:(stl + 1) * 128],
                        rhs=qT[rows, j, q0:q0 + 512], start=True, stop=True),
                        reads=QKALL, writes=[S2K[sl][u]])
                pb = Pb[si % 3]
                P.op("act", lambda e: e.activation(out=pb, in_=S2[sl], func=AF.Exp, scale=0.125),
                     reads=S2K[sl], writes=["Pb%d" % (si % 3)])

            def pair_split(qc, j):
                P.op("dve", lambda e: e.tensor_copy(out=Rhl[0:65, 0, :], in_=Rrow[0:65, :]),
                     reads=["Rrow"], writes=["Rhl"])
                P.op("dve", lambda e: e.tensor_tensor(out=Rhl[0:65, 1, :], in0=Rrow[0:65, :], in1=Rhl[0:65, 0, :],
                                                      op=ALU.subtract),
                     reads=["Rrow", "Rhl"], writes=["Rhl"])

            def pair_bcast(qc, j):
                P.op("pe", lambda e: e.matmul(RBp, lhsT=sel_bf[0:65, :], rhs=Rhl[0:65, 0, :], start=True, stop=False),
                     reads=["cm", "Rhl"], writes=[rbk])
                P.op("pe", lambda e: e.matmul(RBp, lhsT=sel_bf[0:65, :], rhs=Rhl[0:65, 1, :], start=False, stop=True),
                     reads=["cm", "Rhl"], writes=[rbk])
                b2 = j % 2
                q0 = qc * 512
                P.op("dve", lambda e: e.tensor_tensor(out=an[b2][0:64, :], in0=osb[0][0:64, :],
                                                      in1=RBp[0:64, :], op=ALU.mult),
                     reads=[rbk, "osb0"], writes=["an%d" % b2])
                P.op("dve", lambda e: e.tensor_tensor(out=an[b2][64:128, :], in0=osb[1][64:128, :],
                                                      in1=RBp[64:128, :], op=ALU.mult),
                     reads=[rbk, "osb1", "an%d" % b2], writes=["an%d" % b2])
                P.op("dve", lambda e: e.tensor_tensor(out=asq[b2], in0=an[b2], in1=an[b2], op=ALU.mult),
                     reads=["an%d" % b2], writes=["asq%d" % b2])
                P.op("dve", lambda e: e.tensor_scalar(out=attnT[:, j, q0:q0 + 512], in0=an[b2],
                                                      scalar1=col(PR, C_GATT + j), scalar2=None, op0=ALU.mult),
                     reads=["an%d" % b2, "PR"], writes=[("attn", j, qc)])

            def pair_norm(qc, j):
                b2 = j % 2
                q0 = qc * 512
                P.op("pe", lambda e: e.matmul(NA, lhsT=ones_bf, rhs=asq[b2], start=(j == 0), stop=(j == 3)),
                     reads=["cm", "asq%d" % b2], writes=[nak])
                if j == 3:
                    P.op("dve", lambda e: e.tensor_copy(out=r_a[:, q0:q0 + 512], in_=NA),
                         reads=[nak], writes=[("r_a", qc)])
                    pending.append((cur_step["si"] + 5, lambda: qc_rstd(qc)))

            def qc_rstd(qc):
                q0 = qc * 512
                P.op("act", lambda e: e.activation(out=r_a[:, q0:q0 + 512], in_=r_a[:, q0:q0 + 512], func=AF.Ln,
                                                   bias=col(PR, C_EPS), scale=1.0 / 512),
                     reads=[("r_a", qc), "PR"], writes=[("r_a", qc)])
                P.op("act", lambda e: e.activation(out=r_a[:, q0:q0 + 512], in_=r_a[:, q0:q0 + 512], func=AF.Exp,
                                                   scale=-0.5),
                     reads=[("r_a", qc)], writes=[("r_a", qc)])
                pending.append((cur_step["si"] + 3, lambda: qc_scale(qc)))

            def qc_scale(qc):
                q0 = qc * 512
                for j in range(4):
                    P.op("dve", lambda e, j=j: e.tensor_tensor(
                        out=attnT[:, j, q0:q0 + 512], in0=attnT[:, j, q0:q0 + 512], in1=r_a[:, q0:q0 + 512],
                        op=ALU.mult),
                        reads=[("r_a", qc), ("attn", j, qc)], writes=[("attn", j, qc)])

            def emit_pv(si):
                qc, j, stl = steps[si]
                pb = Pb[si % 3]
                P.op("pe", lambda e: e.matmul(OA[0:65, :], lhsT=V0[:, stl, :], rhs=pb[:, 0:512],
                                              start=(stl == 0), stop=(stl == 15)),
                     reads=VALL + ["Pb%d" % (si % 3)], writes=[oak])
                P.op("pe", lambda e: e.matmul(OB, lhsT=V1[:, stl, :], rhs=pb[:, 512:1024],
                                              start=(stl == 0), stop=(stl == 15)),
                     reads=VALL + ["Pb%d" % (si % 3)], writes=[obk])
                if stl == 15:
                    P.op("dve", lambda e: e.tensor_copy(out=osb[0][0:65, :], in_=OA[0:65, :]),
                         reads=[oak], writes=["osb0"])
                    P.op("dve", lambda e: e.tensor_copy(out=osb[1], in_=OB),
                         reads=[obk], writes=["osb1"])
                    P.op("dve", lambda e: e.tensor_copy(out=dn[64:65, :], in_=osb[0][64:65, :]),
                         reads=["osb0"], writes=["dn"])
                    P.op("dve", lambda e: e.tensor_copy(out=dn[0:1, :], in_=osb[1][0:1, :]),
                         reads=["osb1", "dn"], writes=["dn"])
                    P.op("dve", lambda e: e.reciprocal(out=Rrow[0:65, :], in_=dn[0:65, :]),
                         reads=["dn"], writes=["Rrow"])
                    pending.append((si + 6, lambda: pair_split(qc, j)))
                    pending.append((si + 11, lambda: pair_bcast(qc, j)))
                    pending.append((si + 14, lambda: pair_norm(qc, j)))

            cur_step = {"si": 0}

            def conv_unit(cc, tc):
                t0c = tc * 512
                xk_r = [("xl", cc, t) for t in range(4)] + ["xlpad", "PR"]
                dk = ("xc", cc, tc)
                P.op("pool", lambda e: e.tensor_scalar(
                    out=xc[:, cc, t0c:t0c + 512], in0=xlp[:, cc, t0c:t0c + 512],
                    scalar1=col(PR, C_CW + cc * 4 + 0), scalar2=col(PR, C_CB + cc),
                    op0=ALU.mult, op1=ALU.add),
                    reads=xk_r, writes=[dk])
                for jj in range(1, 4):
                    P.op("dve", lambda e, jj=jj: e.scalar_tensor_tensor(
                        out=xc[:, cc, t0c:t0c + 512], in0=xlp[:, cc, t0c + jj:t0c + jj + 512],
                        scalar=col(PR, C_CW + cc * 4 + jj), in1=xc[:, cc, t0c:t0c + 512],
                        op0=ALU.mult, op1=ALU.add), reads=xk_r + [dk], writes=[dk])

            def flush(si):
                cur_step["si"] = min(si, nst)
                progressed = True
                while progressed:
                    progressed = False
                    items = list(pending)
                    pending[:] = []
                    for due, fn in items:
                        if due <= si:
                            fn()
                            progressed = True
                        else:
                            pending.append((due, fn))

            emit_qk(0)
            emit_qk(1)
            for si in range(nst):
                emit_pv(si) if False else None
                if si + 2 < nst:
                    pass
                if si + 2 < nst:
                    emit_qk(si + 2)
                emit_pv(si)
                flush(si)
                if si % 16 == 3:
                    ku = si // 16
                    conv_unit(ku % 4, ku // 4)
            flush(10 ** 9)

            R_D1 = Region(arena, 59392, 86528)
            R_D2 = Region(arena, 107008, ARENA_BYTES)
            R_DX = Region(arena, 86528, 107008)
            w_out_bf = R_DX.alloc([128, 8, D], BF16)
            wgb = [R_DX.alloc([128, 8, 128], BF16), None]
            wub = [R_DX.alloc([128, 8, 128], BF16), None]
            hT = R_D2.alloc([128, 8, 1024], F32)
            ffT = R_D2.alloc([128, NFC, 1024], BF16)
            wdb = [R_D2.alloc([128, NFC, 128], BF16) for _ in range(2)]
            rf = R_D2.alloc([128, 512], F32)
            tgb = [R_D2.alloc([128, 512], F32) for _ in range(2)]
            sgb = [R_D2.alloc([128, 512], F32) for _ in range(2)]
            u2T = R_D1.alloc([128, 8, 1024], BF16)
            wgb[1] = R_D1.alloc([128, 8, 128], BF16)
            wub[1] = R_D1.alloc([128, 8, 128], BF16)
            otile = [R_D1.alloc([128, 512], F32) for _ in range(2)]
            hsq = [R_D1.alloc([128, 512], BF16) for _ in range(2)]
            w_out3 = w_out.rearrange("(k p) n -> p k n", p=128)
            w_gate3 = w_gate.rearrange("(k p) n -> p k n", p=128)
            w_up3 = w_up.rearrange("(k p) n -> p k n", p=128)
            w_down3 = w_down.rearrange("(f p) n -> p f n", p=128)
            XLKEYS = [("xl", cc_, t_) for cc_ in range(4) for t_ in range(4)] + ["xlpad"]

            def prefetch_D(part):
                if part == 0:
                    P.op("pool", lambda e: e.dma_start(out=w_out_bf[:, 0:4, :], in_=w_out3[:, 0:4, :]),
                         writes=["w_outa"] + XLKEYS, dma="w_out0")
                elif part == 1:
                    P.op("pool", lambda e: e.dma_start(out=w_out_bf[:, 4:8, :], in_=w_out3[:, 4:8, :]),
                         writes=["w_outb"] + XLKEYS, dma="w_out1")
                elif part == 2:
                    P.op("pool", lambda e: e.dma_start(out=wgb[0], in_=w_gate3[:, :, 0:128]),
                         writes=["wgb0"] + XLKEYS, dma="wg0")
                elif part == 3:
                    P.op("pool", lambda e: e.dma_start(out=wub[0], in_=w_up3[:, :, 0:128]),
                         writes=["wub0"] + XLKEYS, dma="wu0")

            P.barrier(col(scr, 2))
            R_TMP.reset()
            R_QKV.reset()
            RC = R_QKV
            RL = Region(arena, R_OUTS.start + 2 * 16384 + 8192, R_OUTS.start + 2 * 16384 + 16384)
            xc = R_TMP.alloc([128, 4, S], F32)
            yst = R_TMP.alloc([128, 4, S], F32)
            xcb = [[R_TMP.alloc([128, 512], BF16) for _ in range(2)] for _ in range(2)]
            lsq = [R_TMP.alloc([128, 512], BF16) for _ in range(2)]
            carry = R_TMP.alloc([128, 8], F32)
            RA2 = Region(arena, R_OUTS.start + 2 * 16384, R_OUTS.start + 2 * 16384 + 8192)
            lrub = [[RA2.alloc([128, 512], F32) for _ in range(2)] for _ in range(2)]
            cpend = []
            trb = [RL.alloc([128, 512], F32) for _ in range(2)]
            rlt = [RL.alloc([128, 512], F32) for _ in range(2)]
            ab = [[RC.alloc([128, 512], F32) for _ in range(2)] for _ in range(2)]
            a2b = [[RC.alloc([128, 512], F32) for _ in range(2)] for _ in range(2)]
            tib = [[RC.alloc([128, 512], F32) for _ in range(2)] for _ in range(2)]
            NLB = [(psb[6], PSK[6]), (psb[7], PSK[7])]
            gctr = {"i": 0}

            def stage1(sm, tc, ccs, cast_on_dve=False):
                t0 = tc * 512
                for cc in ccs:
                    h = cc % 2
                    if cast_on_dve:
                        P.op("dve", lambda e, cc=cc, h=h: e.tensor_copy(
                            out=xcb[sm][h], in_=xc[:, cc, t0:t0 + 512]),
                            reads=[("xc", cc, tc)], writes=["xcb%d%d" % (sm, h)])
                    else:
                        P.op("act", lambda e, cc=cc, h=h: e.activation(
                            out=xcb[sm][h], in_=xc[:, cc, t0:t0 + 512], func=AF.Copy),
                            reads=[("xc", cc, tc)], writes=["xcb%d%d" % (sm, h)])
                for cc in ccs:
                    h = cc % 2
                    i = gctr["i"]; gctr["i"] += 1
                    GR, grk = psb[(2 * i) % 6], PSK[(2 * i) % 6]
                    GI, gik = psb[(2 * i + 1) % 6], PSK[(2 * i + 1) % 6]
                    kx = "xcb%d%d" % (sm, h)
                    P.op("pe", lambda e, GR=GR, cc=cc, h=h: e.matmul(
                        GR, lhsT=Wg[:, sm * 8 + cc, :], rhs=xcb[sm][h], start=True, stop=True),
                        reads=["Wg", kx], writes=[grk])
                    P.op("pe", lambda e, GI=GI, cc=cc, h=h: e.matmul(
                        GI, lhsT=Wg[:, sm * 8 + 4 + cc, :], rhs=xcb[sm][h], start=True, stop=True),
                        reads=["Wg", kx], writes=[gik])
                    P.op("act", lambda e, GR=GR, cc=cc: e.activation(
                        out=trb[sm], in_=GR, func=AF.Tanh, bias=col(DV, V_HBR + sm * 4 + cc), scale=0.5),
                        reads=[grk, "DVb"], writes=["trb%d" % sm])
                    P.op("act", lambda e, GI=GI, cc=cc, h=h: e.activation(
                        out=tib[sm][h], in_=GI, func=AF.Tanh, bias=col(DV, V_HBI + sm * 4 + cc), scale=0.5),
                        reads=[gik, "DVb"], writes=["tib%d%d" % (sm, h)])
                    P.op("act", lambda e, cc=cc, h=h: e.activation(
                        out=ab[sm][h], in_=trb[sm], func=AF.Exp, bias=col(DV, V_C1 + sm * 4 + cc),
                        scale=col(DV, V_C1 + sm * 4 + cc)),
                        reads=["trb%d" % sm, "DVc"], writes=["ab%d%d" % (sm, h)])
                    P.op("pool", lambda e, h=h: e.tensor_tensor(
                        out=a2b[sm][h], in0=ab[sm][h], in1=ab[sm][h], op=ALU.mult),
                        reads=["ab%d%d" % (sm, h)], writes=["a2b%d%d" % (sm, h)])
                    P.op("dve", lambda e, cc=cc, h=h: e.scalar_tensor_tensor(
                        out=tib[sm][h], in0=tib[sm][h], scalar=1.0, in1=xc[:, cc, t0:t0 + 512],
                        op0=ALU.add, op1=ALU.mult),
                        reads=["tib%d%d" % (sm, h), ("xc", cc, tc)], writes=["tib%d%d" % (sm, h)])

            def stage2a(sm, ccs):
                for cc in ccs:
                    h = cc % 2
                    k2 = "a2b%d%d" % (sm, h)
                    P.op("act", lambda e, h=h: e.activation(out=a2b[sm][h], in_=a2b[sm][h], func=AF.Sqrt,
                                                            bias=col(DV, V_Q), scale=-0.25),
                         reads=[k2, "DVq"], writes=[k2])

            def stage2b(sm, tc, ccs, first, combine):
                t0 = tc * 512
                deferred = []
                for cc in ccs:
                    h = cc % 2
                    ka, k2, kt = "ab%d%d" % (sm, h), "a2b%d%d" % (sm, h), "tib%d%d" % (sm, h)
                    kc = "carry%d%d" % (sm, cc)
                    cv = carry[:, sm * 4 + cc:sm * 4 + cc + 1]
                    P.op("dve", lambda e, h=h: e.tensor_tensor(
                        out=tib[sm][h], in0=tib[sm][h], in1=a2b[sm][h], op=ALU.mult),
                        reads=[kt, k2], writes=[kt])
                    init = 0.0 if first else cv
                    rk = [] if first else [kc]
                    dst = a2b[sm][h] if combine else yst[:, cc, t0:t0 + 512]
                    dkey = k2 if combine else ("yst", cc, tc)
                    if sm == 0:
                        P.op("dve", lambda e, h=h, dst=dst, init=init: e.tensor_tensor_scan(
                            out=dst, data0=ab[sm][h], data1=tib[sm][h], initial=init, op0=ALU.mult, op1=ALU.add),
                            reads=[ka, kt, k2] + rk, writes=[dkey])
                        P.op("dve", lambda e, dst=dst, cv=cv: e.tensor_copy(out=cv, in_=dst[:, 511:512]),
                             reads=[dkey], writes=[kc])
                    else:
                        P.op("dve", lambda e, h=h, dst=dst, init=init: e.tensor_tensor_scan(
                            out=dst[:, ::-1], data0=ab[sm][h][:, ::-1], data1=tib[sm][h][:, ::-1], initial=init,
                            op0=ALU.mult, op1=ALU.add),
                            reads=[ka, kt, k2] + rk, writes=[dkey])
                        P.op("dve", lambda e, dst=dst, cv=cv: e.tensor_copy(out=cv, in_=dst[:, 0:1]),
                             reads=[dkey], writes=[kc])
                    if combine:
                        cur = a2b[sm][h]
                        lb = lrub[sm][h]
                        kl = "lrub%d%d" % (sm, h)
                        P.op("pool", lambda e, cc=cc, cur=cur: e.tensor_tensor(
                            out=cur, in0=cur, in1=yst[:, cc, t0:t0 + 512], op=ALU.add),
                            reads=[k2, ("yst", cc, tc)], writes=[k2])
                        P.op("pool", lambda e, cc=cc, cur=cur, lb=lb: e.tensor_tensor(
                            out=lb, in0=cur, in1=gg[:, cc, t0:t0 + 512], op=ALU.mult),
                            reads=[k2, ("gg", cc, tc)], writes=[kl])

                        def tail(cc=cc, lb=lb, kl=kl):
                            NL, nlk = NLB[sm]
                            P.op("act", lambda e: e.activation(out=lsq[sm], in_=lb, func=AF.Square),
                                 reads=[kl], writes=["lsq%d" % sm])
                            P.op("pe", lambda e: e.matmul(NL, lhsT=ones_bf, rhs=lsq[sm],
                                                          start=(cc == 0), stop=(cc == 3)),
                                 reads=["cm", "lsq%d" % sm], writes=[nlk])
                            P.op("act", lambda e: e.activation(
                                out=lruT[:, cc, t0:t0 + 512], in_=lb, func=AF.Copy, scale=col(PR, C_GLRU + cc)),
                                reads=[kl, "PR"], writes=[("lru", cc, tc)])
                        cpend.append(tail)

            def cflush():
                while cpend:
                    cpend.pop(0)()

            def finish_tc(sm, tc):
                t0 = tc * 512
                NL, nlk = NLB[sm]
                rstd_from_psum(NL, nlk, 1.0 / 512, rlt[sm], "rlt%d" % sm)
                for cc in range(4):
                    P.op("dve", lambda e, cc=cc: e.tensor_tensor(
                        out=lruT[:, cc, t0:t0 + 512], in0=lruT[:, cc, t0:t0 + 512], in1=rlt[sm], op=ALU.mult),
                        reads=["rlt%d" % sm, ("lru", cc, tc)], writes=[("lru", cc, tc)])

            for p in range(4):
                tcs = (p, 3 - p)
                combine = p >= 2
                for half in range(2):
                    ccs = (2 * half, 2 * half + 1)
                    hi_ = 2 * p + half
                    if 1 <= hi_ <= 4:
                        prefetch_D(hi_ - 1)
                    stage1(0, tcs[0], ccs, cast_on_dve=not combine)
                    stage1(1, tcs[1], ccs, cast_on_dve=not combine)
                    cflush()
                    stage2a(0, ccs)
                    stage2a(1, ccs)
                    stage2b(0, tcs[0], ccs, p == 0, combine)
                    stage2b(1, tcs[1], ccs, p == 0, combine)
                if combine:
                    cpend.append(lambda t_=tcs[0]: finish_tc(0, t_))
                    cpend.append(lambda t_=tcs[1]: finish_tc(1, t_))
            cflush()

            P.barrier(col(scr, 3))
            ii = {"g": 0, "d": 0, "o": 0, "t": 0}
            for hf in range(2):
                h0 = hf * 1024
                for ch in range(2):
                    c0 = ch * 512
                    s0 = h0 + c0
                    g0 = tb + s0
                    tcq = s0 // 512
                    P.op("sp", lambda e, c0=c0, g0=g0: e.dma_start(out=hT[:, :, c0:c0 + 512],
                                                                   in_=xT3[:, :, g0:g0 + 512]),
                         writes=[("h", ch, k) for k in range(8)], dma="xh%d" % ch)
                    for oc in range(8):
                        PA, pak = psum()
                        for j in range(4):
                            P.op("pe", lambda e, PA=PA, j=j, oc=oc, s0=s0: e.matmul(
                                PA, lhsT=w_out_bf[:, j, oc * 128:(oc + 1) * 128], rhs=attnT[:, j, s0:s0 + 512],
                                start=(j == 0), stop=False),
                                reads=["w_outa", ("attn", j, tcq)], writes=[pak])
                        for cc in range(4):
                            P.op("pe", lambda e, PA=PA, cc=cc, oc=oc, s0=s0: e.matmul(
                                PA, lhsT=w_out_bf[:, 4 + cc, oc * 128:(oc + 1) * 128], rhs=lruT[:, cc, s0:s0 + 512],
                                start=False, stop=(cc == 3)),
                                reads=["w_outb", ("lru", cc, tcq)], writes=[pak])
                        hk = ("h", ch, oc)
                        hv = hT[:, oc, c0:c0 + 512]
                        P.op("dve", lambda e, PA=PA, hv=hv: e.tensor_tensor(out=hv, in0=PA, in1=hv, op=ALU.add),
                             reads=[pak, hk], writes=[hk])
                    NF, nfk = psum()
                    for k in range(8):
                        hs = hsq[k % 2]
                        P.op("act", lambda e, k=k, hs=hs, c0=c0: e.activation(
                            out=hs, in_=hT[:, k, c0:c0 + 512], func=AF.Square),
                            reads=[("h", ch, k)], writes=["hsq%d" % (k % 2)])
                        P.op("pe", lambda e, k=k, hs=hs, NF=NF: e.matmul(NF, lhsT=ones_bf, rhs=hs,
                                                                         start=(k == 0), stop=(k == 7)),
                             reads=["cm", "hsq%d" % (k % 2)], writes=[nfk])
                    rstd_from_psum(NF, nfk, 1.0 / D, rf, "rf")
                    for k in range(8):
                        P.op("dve", lambda e, k=k, c0=c0: e.scalar_tensor_tensor(
                            out=u2T[:, k, c0:c0 + 512], in0=hT[:, k, c0:c0 + 512], scalar=col(PR, C_GFFN + k),
                            in1=rf, op0=ALU.mult, op1=ALU.mult),
                            reads=[("h", ch, k), "rf", "PR"], writes=[("u2", ch, k)])
                for fc in range(NFC):
                    sl = ii["g"] % 2; ii["g"] += 1
                    if not (hf == 0 and fc == 0):
                        P.op("pool", lambda e, fc=fc, sl=sl: e.dma_start(
                            out=wgb[sl], in_=w_gate3[:, :, fc * 128:(fc + 1) * 128]),
                            writes=["wgb%d" % sl], dma="wg%d" % sl)
                        P.op("pool", lambda e, fc=fc, sl=sl: e.dma_start(
                            out=wub[sl], in_=w_up3[:, :, fc * 128:(fc + 1) * 128]),
                            writes=["wub%d" % sl], dma="wu%d" % sl)
                    for ch in range(2):
                        c0 = ch * 512
                        G, gk_ = psum()
                        U, uk_ = psum()
                        u2k = [("u2", ch, k) for k in range(8)]
                        for k in range(8):
                            P.op("pe", lambda e, G=G, k=k, sl=sl, c0=c0: e.matmul(
                                G, lhsT=wgb[sl][:, k, :], rhs=u2T[:, k, c0:c0 + 512], start=(k == 0), stop=(k == 7)),
                                reads=["wgb%d" % sl] + u2k, writes=[gk_])
                        for k in range(8):
                            P.op("pe", lambda e, U=U, k=k, sl=sl, c0=c0: e.matmul(
                                U, lhsT=wub[sl][:, k, :], rhs=u2T[:, k, c0:c0 + 512], start=(k == 0), stop=(k == 7)),
                                reads=["wub%d" % sl] + u2k, writes=[uk_])
                        i = ii["o"]; ii["o"] += 1
                        b2 = i % 2
                        P.op("act", lambda e, G=G, b2=b2: e.activation(out=tgb[b2], in_=G, func=AF.Tanh, scale=0.5),
                             reads=[gk_], writes=["tgb%d" % b2])
                        P.op("dve", lambda e, G=G, b2=b2: e.scalar_tensor_tensor(
                            out=sgb[b2], in0=tgb[b2], scalar=1.0, in1=G, op0=ALU.add, op1=ALU.mult),
                            reads=["tgb%d" % b2, gk_], writes=["sgb%d" % b2])
                        P.op("dve", lambda e, U=U, b2=b2, fc=fc, c0=c0: e.scalar_tensor_tensor(
                            out=ffT[:, fc, c0:c0 + 512], in0=sgb[b2], scalar=0.5, in1=U, op0=ALU.mult, op1=ALU.mult),
                            reads=["sgb%d" % b2, uk_], writes=[("ff", ch, fc)])
                for oc in range(8):
                    sl = ii["d"] % 2; ii["d"] += 1
                    P.op("pool", lambda e, oc=oc, sl=sl: e.dma_start(
                        out=wdb[sl], in_=w_down3[:, :, oc * 128:(oc + 1) * 128]),
                        writes=["wdb%d" % sl], dma="wd%d" % sl)
                    for ch in range(2):
                        c0 = ch * 512
                        g0 = tb + h0 + c0
                        Dn, dk_ = psum()
                        ffk = [("ff", ch, fc) for fc in range(NFC)]
                        for fc in range(NFC):
                            P.op("pe", lambda e, Dn=Dn, fc=fc, sl=sl, c0=c0: e.matmul(
                                Dn, lhsT=wdb[sl][:, fc, :], rhs=ffT[:, fc, c0:c0 + 512],
                                start=(fc == 0), stop=(fc == NFC - 1)),
                                reads=["wdb%d" % sl] + ffk, writes=[dk_])
                        i = ii["o"]; ii["o"] += 1
                        ot = otile[i % 2]
                        P.op("dve", lambda e, Dn=Dn, ot=ot, oc=oc, c0=c0: e.tensor_tensor(
                            out=ot, in0=Dn, in1=hT[:, oc, c0:c0 + 512], op=ALU.add),
                            reads=[dk_, ("h", ch, oc)], writes=["ot%d" % (i % 2)])
                        P.op("sp", lambda e, ot=ot, oc=oc, g0=g0: e.dma_start(
                            out=outT3[:, oc, g0:g0 + 512], in_=ot),
                            reads=["ot%d" % (i % 2)], dma="st%d" % (i % 2))

        P.emit(st)
        build.stats = P.stats
    return nc


def _host_consts():
    rows = S // 64
    row = np.repeat(np.arange(rows, dtype=np.float32), 64)
    colp = np.tile(np.arange(64, dtype=np.float32), rows)
    inv = (np.float32(10000.0) ** (-np.arange(0, 32, 2, dtype=np.float32) / np.float32(32))).astype(np.float32)
    ang_r = row[:, None] * inv[None, :]
    ang_c = colp[:, None] * inv[None, :]
    ang = np.concatenate([ang_r, ang_r, ang_c, ang_c], axis=-1).astype(np.float32)
    cos = np.cos(ang).astype(np.float32).T
    sin = np.sin(ang).astype(np.float32).T
    d = np.arange(64)
    sign = np.where(d % 32 < 16, -1.0, 1.0).astype(np.float32)[:, None]
    sin_s = sin * sign
    rope = np.zeros((128, 2, S), np.float32)
    rope[:, 0, :] = np.concatenate([cos, cos], axis=0)
    rope[:, 1, :] = np.concatenate([sin_s, sin_s], axis=0)
    cm = np.zeros((128, 4, 128), np.float32)
    cm[:, 0, :] = 1.0
    for b in range(2):
        cm[b * 64:(b + 1) * 64, 1, b * 64:(b + 1) * 64] = 1.0 / 64.0
    m = np.arange(128)
    src = np.where(m % 32 < 16, m + 16, m - 16)
    cm[src, 2, m] = 1.0
    cm[64, 3, 0:64] = 1.0
    cm[0, 3, 64:128] = 1.0
    return rope, cm


def kernel(x, norm_mix, w_in, q_norm, k_norm, conv_w, conv_b, w_rgate, b_rgate,
           w_igate, b_igate, lru_lambda, out_norm_attn, out_norm_lru, w_out,
           norm_ffn, w_gate, w_up, w_down):
    f32 = np.float32
    x = np.asarray(x, f32)
    B = x.shape[0]
    assert B == N_CORES * NSEQ
    rope, cm = _host_consts()
    hperm = np.concatenate([np.arange(64) + 64 * h for j in range(4) for h in (j, 4 + j)])
    w_in0 = np.asarray(w_in, f32)[0]
    w_in_p = np.ascontiguousarray(np.concatenate([w_in0[:, hperm], w_in0[:, 512:]], axis=1))
    w_out0 = np.asarray(w_out, f32)[0]
    w_out_p = np.ascontiguousarray(np.concatenate([w_out0[hperm, :], w_out0[512:, :]], axis=0))
    prm = np.zeros((128, 72), f32)
    prm[:, 0:8] = np.asarray(norm_mix, f32)[0].reshape(8, 128).T
    prm[:, 8:16] = np.asarray(norm_ffn, f32)[0].reshape(8, 128).T
    prm[:, 16] = np.tile(np.asarray(q_norm, f32)[0], 2)
    prm[:, 17] = np.tile(np.asarray(k_norm, f32)[0], 2)
    prm[:, 18:22] = np.asarray(out_norm_attn, f32)[0][hperm].reshape(4, 128).T
    prm[:, 22:26] = np.asarray(out_norm_lru, f32)[0].reshape(4, 128).T
    cw = np.asarray(conv_w, f32)[0]
    for cc in range(4):
        for j in range(4):
            prm[:, 26 + cc * 4 + j] = cw[j, cc * 128:(cc + 1) * 128]
    prm[:, 42:46] = np.asarray(conv_b, f32)[0].reshape(4, 128).T
    for dr in range(2):
        prm[:, 46 + dr * 4:50 + dr * 4] = np.asarray(b_rgate, f32)[0, dr].reshape(4, 128).T
        prm[:, 54 + dr * 4:58 + dr * 4] = np.asarray(b_igate, f32)[0, dr].reshape(4, 128).T
        prm[:, 62 + dr * 4:66 + dr * 4] = np.asarray(lru_lambda, f32)[0, dr].reshape(4, 128).T
    prm[:, 70] = EPS
    prm[:, 71] = 1.0
    shared = {
        "w_in": w_in_p, "w_out": w_out_p,
        "w_gate": np.ascontiguousarray(np.asarray(w_gate, f32)[0]),
        "w_up": np.ascontiguousarray(np.asarray(w_up, f32)[0]),
        "w_down": np.ascontiguousarray(np.asarray(w_down, f32)[0]),
        "w_rg": np.ascontiguousarray(np.asarray(w_rgate, f32)[0]),
        "w_ig": np.ascontiguousarray(np.asarray(w_igate, f32)[0]),
        "prm": prm, "cmat": cm, "rope": rope,
    }
    in_maps = []
    for c in range(N_CORES):
        xs = x[c * NSEQ:(c + 1) * NSEQ]
        xTc = np.ascontiguousarray(xs.transpose(2, 0, 1).reshape(D, NT))
        m = dict(shared)
        m["xT"] = xTc
        in_maps.append(m)
    nc = build()
    res = run_bass_kernel_spmd(nc, in_maps, core_ids=list(range(N_CORES)))
    out = np.empty((B, S, D), f32)
    for c in range(N_CORES):
        oT = np.asarray(res.results[c]["outT"], f32)
        out[c * NSEQ:(c + 1) * NSEQ] = oT.reshape(D, NSEQ, S).transpose(1, 2, 0)
    return out
```

```python
from contextlib import ExitStack

import numpy as np
import concourse.bass as bass
import concourse.mybir as mybir
from concourse.bass_utils import run_bass_kernel_spmd

F32 = mybir.dt.float32
BF16 = mybir.dt.bfloat16
ALU = mybir.AluOpType
AF = mybir.ActivationFunctionType
COMPUTE = ("pe", "act", "dve", "pool")

N_CORES = 8
S = 2048
NSEQ = 2
NT = NSEQ * S
D = 1024
DFF = 2816
NFC = DFF // 128
EPS = 1e-6


class Prog:
    def __init__(self, nc, same_engine_sync=("act", "dve", "pool")):
        self.nc = nc
        self.ops = []
        self.last_writer = {}
        self.readers = {}
        self.same_engine_sync = set(same_engine_sync)
        self.dma_groups = {}
        self.epoch_op = None
        self.last_on_eng = {}

    def op(self, eng, fn, reads=(), writes=(), dma=None, extra_deps=()):
        deps = set(extra_deps)
        pr = [k for k in reads if isinstance(k, str) and k.startswith("ps")]
        if pr:
            reads = [k for k in reads if k not in pr]
            writes = list(writes) + [k for k in pr if k not in writes]
        for k in reads:
            w = self.last_writer.get(k)
            if w is not None:
                deps.add(w)
        for k in writes:
            w = self.last_writer.get(k)
            if w is not None:
                deps.add(w)
            deps.update(self.readers.get(k, ()))
        if self.epoch_op is not None:
            deps.add(self.epoch_op)
        red = {}
        for d_ in deps:
            do = self.ops[d_]
            kk = do["eng"] if do["dma"] is None else ("dma", do["dma"])
            if d_ > red.get(kk, -1):
                red[kk] = d_
        deps = set(red.values())
        oid = len(self.ops)
        self.ops.append(dict(eng=eng, fn=fn, deps=deps, dma=dma))
        if dma is not None:
            self.dma_groups.setdefault(dma, []).append(oid)
            self.last_on_eng[("dma", dma)] = oid
        else:
            self.last_on_eng[eng] = oid
        for k in reads:
            self.readers.setdefault(k, []).append(oid)
        for k in writes:
            self.last_writer[k] = oid
            self.readers[k] = []
        return oid

    def barrier(self, scratch):
        deps = set(self.last_on_eng.values())
        self.epoch_op = None
        oid = self.op("dve", lambda e: e.memset(scratch, 0.0), extra_deps=deps)
        self.epoch_op = oid
        return oid

    def emit(self, stack, final_wait_engine="sp"):
        nc = self.nc
        ops = self.ops
        for o in ops:
            nd = set()
            for d in o["deps"]:
                do = ops[d]
                if do["dma"] is None and o["dma"] is None and do["eng"] == o["eng"]:
                    if o["eng"] not in self.same_engine_sync:
                        continue
                nd.add(d)
            o["deps"] = nd
        marked = set()
        for o in ops:
            for d in o["deps"]:
                if ops[d]["dma"] is None:
                    marked.add(d)
        esem = {e: stack.enter_context(nc.semaphore("s_" + e)) for e in COMPUTE}
        gsem = {g: stack.enter_context(nc.semaphore("g_" + str(g))) for g in self.dma_groups}
        cnt = {e: 0 for e in COMPUTE}
        for i, o in enumerate(ops):
            if o["dma"] is None and i in marked:
                cnt[o["eng"]] += 1
                o["val"] = cnt[o["eng"]]
        import bisect

        def dep_wait(dep_id, user_id):
            do = ops[dep_id]
            if do["dma"] is None:
                return esem[do["eng"]], do["val"]
            g = self.dma_groups[do["dma"]]
            n = bisect.bisect_left(g, user_id)
            return gsem[do["dma"]], 16 * n

        per_eng = {}
        for i, o in enumerate(ops):
            per_eng.setdefault(o["eng"], []).append(i)
        final = [(gsem[g], 16 * len(lst)) for g, lst in self.dma_groups.items()]
        block = stack.enter_context(nc.Block())
        reg = {"pe": block.tensor, "act": block.scalar, "dve": block.vector,
               "pool": block.gpsimd, "sp": block.sync}
        if final_wait_engine not in per_eng:
            per_eng[final_wait_engine] = []
        stats = dict(waits=0)

        def run_engine(e, lst, engine):
            waited = {}
            for i in lst:
                o = ops[i]
                need = {}
                for d in o["deps"]:
                    s, v = dep_wait(d, i)
                    if v > need.get(s.num, (None, 0))[1]:
                        need[s.num] = (s, v)
                todo = [(s, v) for key, (s, v) in need.items() if waited.get(key, 0) < v]
                embed = None
                if todo and e != "pe":
                    embed = todo.pop()
                for s, v in todo:
                    engine.wait_ge(s, v)
                    waited[s.num] = v
                    stats["waits"] += 1
                ins = o["fn"](engine)
                if embed is not None:
                    ins._wait_ge(embed[0], embed[1])
                    waited[embed[0].num] = embed[1]
                if o["dma"] is not None:
                    ins.then_inc(gsem[o["dma"]], 16)
                elif i in marked:
                    ins.then_inc(esem[e], 1)
            if e == final_wait_engine:
                for s, v in final:
                    if waited.get(s.num, 0) < v:
                        engine.wait_ge(s, v)

        for e, lst in per_eng.items():
            def f(engine, e=e, lst=lst):
                run_engine(e, lst, engine)
            reg[e](f)
        self.stats = dict(n_ops=len(ops), n_waits=stats["waits"], marked=len(marked),
                          per_eng={e: len(l) for e, l in per_eng.items()})


class Region:
    def __init__(self, arena, start, end):
        self.arena, self.start, self.end, self.ptr = arena, start, end, start

    def reset(self):
        self.ptr = self.start

    def alloc(self, shape, dt):
        n = 1
        for s in shape[1:]:
            n *= s
        nbytes = n * (4 if dt == F32 else 2)
        nbytes_al = (nbytes + 31) // 32 * 32
        off = self.ptr
        assert off + nbytes_al <= self.end, (off, nbytes_al, self.end)
        self.ptr += nbytes_al
        v = self.arena[:, off // 2:(off + nbytes) // 2]
        if dt == F32:
            v = v.bitcast(F32)
        if len(shape) == 3:
            v = v.rearrange("p (a b) -> p a b", a=shape[1])
        return v


def build(debug=False):
    nc = bass.Bass("TRN2", target_bir_lowering=False)

    def din(name, shape):
        return nc.dram_tensor(name, list(shape), F32, kind="ExternalInput").ap()

    xT = din("xT", [D, NT])
    w_in = din("w_in", [D, 1792])
    w_out = din("w_out", [D, D])
    w_gate = din("w_gate", [D, DFF])
    w_up = din("w_up", [D, DFF])
    w_down = din("w_down", [DFF, D])
    w_rg = din("w_rg", [2, 8, 64, 64])
    w_ig = din("w_ig", [2, 8, 64, 64])
    prm = din("prm", [128, 72])
    cmat = din("cmat", [128, 4, 128])
    rope = din("rope", [128, 2, S])
    outT = nc.dram_tensor("outT", [D, NT], F32, kind="ExternalOutput").ap()

    xT3 = xT.rearrange("(k p) t -> p k t", p=128)
    outT3 = outT.rearrange("(k p) t -> p k t", p=128)

    with ExitStack() as st:
        ARENA_BYTES = 212480
        arena = st.enter_context(nc.sbuf_tensor("arena", [128, ARENA_BYTES // 2], BF16))
        ps_all = st.enter_context(nc.psum_tensor("ps_all", [128, 4096], F32))[:]
        psb = [ps_all[:, i * 512:(i + 1) * 512] for i in range(8)]
        PSK = ["ps%d" % i for i in range(8)]

        R_CONST = Region(arena, 0, 10240)
        R_OUTS = Region(arena, 10240, 59392)
        R_QKV = Region(arena, 59392, 86528)
        R_XG = Region(arena, 86528, 135808)
        R_TMP = Region(arena, 135808, ARENA_BYTES)
        R_D = Region(arena, 59392, ARENA_BYTES)

        P = Prog(nc)

        PR = R_CONST.alloc([128, 72], F32)
        DV = R_CONST.alloc([128, 40], F32)
        cm_bf = R_CONST.alloc([128, 4, 128], BF16)
        Wg = R_CONST.alloc([128, 16, 128], BF16)
        scr = R_CONST.alloc([128, 8], F32)
        Rrow = R_CONST.alloc([128, 512], F32)
        Rhl = R_CONST.alloc([128, 2, 512], BF16)
        ones_bf = cm_bf[:, 0, :]
        blk_bf = cm_bf[:, 1, :]
        perm_bf = cm_bf[:, 2, :]
        sel_bf = cm_bf[:, 3, :]
        C_GMIX, C_GFFN, C_GQ, C_GK, C_GATT, C_GLRU = 0, 8, 16, 17, 18, 22
        C_CW, C_CB, C_BR, C_BI, C_LAM, C_EPS, C_ONE = 26, 42, 46, 54, 62, 70, 71
        V_HBR, V_HBI, V_C1, V_C2, V_Q = 0, 8, 16, 24, 32

        def col(t, c):
            return t[:, c:c + 1]

        P.op("sp", lambda e: e.dma_start(out=PR, in_=prm), writes=["PR"], dma="c_prm")
        P.op("pool", lambda e: e.dma_start(out=cm_bf, in_=cmat), writes=["cm"], dma="c_cm")
        P.op("dve", lambda e: e.memset(Wg.rearrange("p a b -> p (a b)"), 0.0), writes=["Wg"])
        P.op("dve", lambda e: e.memset(Rrow, 0.0), writes=["Rrow"])
        P.op("dve", lambda e: e.memset(Rhl.rearrange("p a b -> p (a b)"), 0.0), writes=["Rhl"])
        for dr in range(2):
            for gi, wsrc in enumerate((w_rg, w_ig)):
                wv = wsrc[dr].rearrange("(cc h) c d -> h c cc d", h=2)
                for h in range(2):
                    idx = dr * 8 + gi * 4
                    P.op("pool", lambda e, wv=wv, h=h, idx=idx: e.dma_start(
                        out=Wg[h * 64:(h + 1) * 64, idx:idx + 4, h * 64:(h + 1) * 64], in_=wv[h]),
                        reads=[], writes=["Wg"], dma="c_wg")
        P.op("dve", lambda e: e.tensor_scalar(out=DV[:, V_HBR:V_HBR + 16], in0=PR[:, C_BR:C_BR + 16],
                                              scalar1=0.5, scalar2=None, op0=ALU.mult),
             reads=["PR"], writes=["DVb"])
        P.op("act", lambda e: e.activation(out=DV[:, V_C1:V_C1 + 8], in_=PR[:, C_LAM:C_LAM + 8],
                                           func=AF.Exp, scale=-1.0), reads=["PR"], writes=["DVc"])
        P.op("act", lambda e: e.activation(out=DV[:, V_C1:V_C1 + 8], in_=DV[:, V_C1:V_C1 + 8],
                                           func=AF.Ln, bias=col(PR, C_ONE), scale=1.0),
             reads=["PR", "DVc"], writes=["DVc"])
        P.op("dve", lambda e: e.tensor_scalar(out=DV[:, V_C2:V_C2 + 8], in0=DV[:, V_C1:V_C1 + 8],
                                              scalar1=-8.0, scalar2=None, op0=ALU.mult),
             reads=["DVc"], writes=["DVc2"])
        P.op("dve", lambda e: e.tensor_scalar(out=DV[:, V_C1:V_C1 + 8], in0=DV[:, V_C1:V_C1 + 8],
                                              scalar1=-4.0, scalar2=None, op0=ALU.mult),
             reads=["DVc"], writes=["DVc"])
        P.op("dve", lambda e: e.memset(DV[:, V_Q:V_Q + 1], 0.25), writes=["DVq"])

        attnT = R_OUTS.alloc([128, 4, S], BF16)
        lruT = R_OUTS.alloc([128, 4, S], BF16)
        r_a = R_OUTS.alloc([128, S], F32)
        r_l = R_OUTS.alloc([128, S], F32)
        outs_end = R_OUTS.ptr
        R_OUTS_A = Region(arena, 10240, 59392)

        qT = R_QKV.alloc([128, 4, S], BF16)
        kT = R_QKV.alloc([128, S], BF16)
        V0 = R_QKV.alloc([128, 16, 65], BF16)
        V1 = R_QKV.alloc([128, 16, 128], BF16)
        xlp = R_XG.alloc([128, 4, S + 4], F32)
        gg = R_XG.alloc([128, 4, S], BF16)

        rr = {"ps": 0}

        def psum():
            i = rr["ps"] % 8
            rr["ps"] += 1
            return psb[i], PSK[i]

        def rstd_from_psum(bank, bk, scale, out_ap, out_key):
            P.op("act", lambda e: e.activation(out=out_ap, in_=bank, func=AF.Ln,
                                               bias=col(PR, C_EPS), scale=scale),
                 reads=[bk, "PR"], writes=[out_key])
            P.op("act", lambda e: e.activation(out=out_ap, in_=out_ap, func=AF.Exp, scale=-0.5),
                 reads=[out_key], writes=[out_key])

        warm_rhs = Wg.rearrange("p a b -> p (a b)")[:, 0:512]

        def warm(n=24):
            for _ in range(n):
                P.op("pe", lambda e: e.matmul(psb[7], lhsT=ones_bf, rhs=warm_rhs, start=True, stop=True),
                     reads=["cm", "Wg"], writes=[PSK[7]])

        for sq in range(NSEQ):
            tb = sq * S
            P.barrier(col(scr, 0))
            warm()
            R_TMP.reset()
            RA = R_OUTS_A
            RA.reset()
            w_in_bf = R_TMP.alloc([128, 8, 1792], BF16)
            ropet = R_TMP.alloc([128, 2, S], F32)
            xin = R_TMP.alloc([128, 8, 512], F32)
            uTb = [R_TMP.alloc([128, 8, 512], BF16), RA.alloc([128, 8, 512], BF16)]
            xsq = [R_TMP.alloc([128, 512], BF16) for _ in range(6)] + [RA.alloc([128, 512], BF16) for _ in range(2)]
            rb = RA.alloc([128, 512], F32)
            NB3 = 3
            sqb = [RA.alloc([128, 512], BF16) for _ in range(NB3)]
            rsb = [RA.alloc([128, 512], F32) for _ in range(NB3)]
            qnb = [RA.alloc([128, 512], BF16) for _ in range(NB3)]
            t1b = [RA.alloc([128, 512], F32) for _ in range(NB3)]
            t2b = [RA.alloc([128, 512], F32) for _ in range(NB3)]
            z2b = [RA.alloc([128, 512], F32) for _ in range(2)]
            zhb = [RA.alloc([128, 512], F32) for _ in range(2)]
            thb = [RA.alloc([128, 512], F32) for _ in range(2)]

            w_in3 = w_in.rearrange("(k p) n -> p k n", p=128)
            for wi, (c0_, c1_) in enumerate(((0, 640), (640, 768), (768, 1280), (1280, 1792))):
                P.op("pool", lambda e, c0_=c0_, c1_=c1_: e.dma_start(out=w_in_bf[:, :, c0_:c1_], in_=w_in3[:, :, c0_:c1_]),
                     writes=["w_in%d" % wi], dma="w_in%d" % wi)
            P.op("pool", lambda e: e.memset(V1.rearrange("p a b -> p (a b)"), 0.0), writes=["V1c"])
            P.op("pool", lambda e: e.memset(V1[:, :, 0:1], 1.0), writes=["V1c"])
            P.op("pool", lambda e: e.memset(V0[:, :, 64:65], 1.0), writes=["V0c"])
            P.op("pool", lambda e: e.memset(xlp[:, :, 0:1], 0.0), writes=["xlpad"])
            P.op("pool", lambda e: e.memset(xlp[:, :, S + 1:S + 4], 0.0), writes=["xlpad"])

            def pro1(tc, ks=range(8), dma=True):
                g0 = tb + tc * 512
                if dma:
                    P.op("sp", lambda e: e.dma_start(out=xin[:, 0:4, :], in_=xT3[:, 0:4, g0:g0 + 512]),
                         writes=["xinA"], dma="xinA")
                    P.op("sp", lambda e: e.dma_start(out=xin[:, 4:8, :], in_=xT3[:, 4:8, g0:g0 + 512]),
                         writes=["xinB"], dma="xinB")
                for k in ks:
                    P.op("act", lambda e, k=k: e.activation(out=xsq[k], in_=xin[:, k, :], func=AF.Square),
                         reads=["xinA" if k < 4 else "xinB"], writes=["xsq%d" % k])

            pro_state = {}

            def pro2(tc):
                ssb, ssk = psum()
                pro_state[tc] = (ssb, ssk)
                for k in range(8):
                    P.op("pe", lambda e, k=k: e.matmul(ssb, lhsT=ones_bf, rhs=xsq[k], start=(k == 0), stop=(k == 7)),
                         reads=["cm", "xsq%d" % k], writes=[ssk])
                rstd_from_psum(ssb, ssk, 1.0 / D, rb, "rb")

            def pro3(tc, ks=range(8)):
                uT = uTb[tc % 2]
                for k in ks:
                    P.op("dve", lambda e, k=k: e.scalar_tensor_tensor(
                        out=uT[:, k, :], in0=xin[:, k, :], scalar=col(PR, C_GMIX + k), in1=rb,
                        op0=ALU.mult, op1=ALU.mult),
                        reads=["xinA" if k < 4 else "xinB", "rb", "PR"], writes=[("uT", tc % 2, k)])

            cnt = {"i": 0, "g": 0}

            def make_tasks(tc):
                t0 = tc * 512
                uT = uTb[tc % 2]
                UTK = [("uT", tc % 2, k) for k in range(8)]
                tasks = []

                def proj(fc0, bank, bk):
                    WINK = ["w_in%d" % (0 if fc0 < 640 else (1 if fc0 < 768 else (2 if fc0 < 1280 else 3)))]
                    for k in range(8):
                        P.op("pe", lambda e, k=k: e.matmul(
                            bank, lhsT=w_in_bf[:, k, fc0:fc0 + 128], rhs=uT[:, k, :],
                            start=(k == 0), stop=(k == 7)),
                            reads=UTK + WINK, writes=[bk])

                def qk_task(fc):
                    i = cnt["i"]; cnt["i"] += 1
                    b3 = i % NB3
                    stt = {}

                    def s1():
                        X, xk = psum()
                        stt["X"] = (X, xk)
                        proj(fc * 128, X, xk)
                        P.op("act", lambda e: e.activation(out=sqb[b3], in_=X, func=AF.Square),
                             reads=[xk], writes=["sqb%d" % b3])

                    def s2():
                        X, xk = stt["X"]
                        Y, yk = psum()
                        P.op("pe", lambda e: e.matmul(Y, lhsT=blk_bf, rhs=sqb[b3], start=True, stop=True),
                             reads=["cm", "sqb%d" % b3], writes=[yk])
                        rstd_from_psum(Y, yk, 1.0, rsb[b3], "rsb%d" % b3)
                        gcol = C_GQ if fc < 4 else C_GK
                        P.op("dve", lambda e: e.scalar_tensor_tensor(
                            out=qnb[b3], in0=X, scalar=col(PR, gcol), in1=rsb[b3], op0=ALU.mult, op1=ALU.mult),
                            reads=[xk, "rsb%d" % b3, "PR"], writes=["qnb%d" % b3])

                    def s3():
                        Z, zk = psum()
                        P.op("pe", lambda e: e.matmul(Z, lhsT=perm_bf, rhs=qnb[b3], start=True, stop=True),
                             reads=["cm", "qnb%d" % b3], writes=[zk])
                        P.op("dve", lambda e: e.tensor_tensor(
                            out=t1b[b3], in0=qnb[b3], in1=ropet[:, 0, t0:t0 + 512], op=ALU.mult),
                            reads=["qnb%d" % b3, "rope"], writes=["t1b%d" % b3])
                        P.op("dve", lambda e: e.tensor_tensor(
                            out=t2b[b3], in0=Z, in1=ropet[:, 1, t0:t0 + 512], op=ALU.mult),
                            reads=[zk, "rope"], writes=["t2b%d" % b3])
                        dst = qT[:, fc, t0:t0 + 512] if fc < 4 else kT[:, t0:t0 + 512]
                        P.op("pool", lambda e: e.tensor_tensor(out=dst, in0=t1b[b3], in1=t2b[b3], op=ALU.add),
                             reads=["t1b%d" % b3, "t2b%d" % b3], writes=[("qk", fc, tc)])
                    return (s1, s2, s3)

                def v_task():
                    def s1():
                        VB, vk = psum()
                        for j in range(4):
                            for k in range(8):
                                P.op("pe", lambda e, j=j, k=k: e.matmul(
                                    VB[:, j * 128:(j + 1) * 128], lhsT=uT[:, k, j * 128:(j + 1) * 128],
                                    rhs=w_in_bf[:, k, 640:768], start=(k == 0), stop=(k == 7)),
                                    reads=UTK + ["w_in1"], writes=[vk])
                        VB3 = VB.rearrange("p (a b) -> p a b", a=4)
                        P.op("act", lambda e: e.activation(
                            out=V0[:, tc * 4:tc * 4 + 4, 0:64], in_=VB3[:, :, 0:64], func=AF.Copy),
                            reads=[vk, "V0c"], writes=[("V0", tc)])
                        P.op("dve", lambda e: e.tensor_copy(
                            out=V1[:, tc * 4:tc * 4 + 4, 64:128], in_=VB3[:, :, 64:128]),
                            reads=[vk, "V1c"], writes=[("V1", tc)])
                    return (s1, None, None)

                def xl_task(cc):
                    def s1():
                        X, xk = psum()
                        proj(768 + cc * 128, X, xk)
                        P.op("act", lambda e: e.activation(
                            out=xlp[:, cc, 1 + t0:1 + t0 + 512], in_=X, func=AF.Copy),
                            reads=[xk, "xlpad"], writes=[("xl", cc, tc)])
                    return (s1, None, None)

                def gl_task(cc):
                    i = cnt["g"]; cnt["g"] += 1
                    b2 = i % 2

                    def s1():
                        X, xk = psum()
                        proj(1280 + cc * 128, X, xk)
                        P.op("act", lambda e: e.activation(out=z2b[b2], in_=X, func=AF.Square),
                             reads=[xk], writes=["z2b%d" % b2])
                        P.op("act", lambda e: e.activation(out=zhb[b2], in_=X, func=AF.Copy, scale=0.5),
                             reads=[xk], writes=["zhb%d" % b2])
                        P.op("dve", lambda e: e.tensor_scalar(
                            out=z2b[b2], in0=z2b[b2], scalar1=0.044715, scalar2=1.0, op0=ALU.mult, op1=ALU.add),
                            reads=["z2b%d" % b2], writes=["z2b%d" % b2])
                        P.op("dve", lambda e: e.tensor_tensor(
                            out=z2b[b2], in0=z2b[b2], in1=zhb[b2], op=ALU.mult),
                            reads=["z2b%d" % b2, "zhb%d" % b2], writes=["z2b%d" % b2])

                    def s2():
                        P.op("act", lambda e: e.activation(
                            out=thb[b2], in_=z2b[b2], func=AF.Tanh, scale=2.0 * 0.7978845608028654),
                            reads=["z2b%d" % b2], writes=["thb%d" % b2])
                        P.op("dve", lambda e: e.scalar_tensor_tensor(
                            out=gg[:, cc, t0:t0 + 512], in0=thb[b2], scalar=1.0, in1=zhb[b2],
                            op0=ALU.add, op1=ALU.mult),
                            reads=["thb%d" % b2, "zhb%d" % b2], writes=[("gg", cc, tc)])
                    return (s1, s2, None)

                for fc in range(5):
                    tasks.append(qk_task(fc))
                tasks.append(v_task())
                for cc in range(4):
                    tasks.append(xl_task(cc))
                for cc in range(4):
                    tasks.append(gl_task(cc))
                return tasks

            pro1(0)
            P.op("sp", lambda e: e.dma_start(out=ropet, in_=rope), writes=["rope"], dma="rope")
            pro2(0)
            pro3(0)
            for tc in range(4):
                tasks = make_tasks(tc)
                n = len(tasks)
                for i in range(n + 3):
                    if i < n:
                        tasks[i][0]()
                    if 0 <= i - 1 < n and tasks[i - 1][1] is not None:
                        tasks[i - 1][1]()
                    if 0 <= i - 3 < n and tasks[i - 3][2] is not None:
                        tasks[i - 3][2]()
                    if tc + 1 < 4:
                        if 1 <= i <= 4:
                            pro1(tc + 1, ks=(2 * (i - 1), 2 * (i - 1) + 1), dma=(i == 1))
                        if i == 6:
                            pro2(tc + 1)
                        if 8 <= i <= 11:
                            pro3(tc + 1, ks=(2 * (i - 8), 2 * (i - 8) + 1))

            P.barrier(col(scr, 1))
            warm()
            R_TMP.reset()
            xc = R_TMP.alloc([128, 4, S], F32)
            Pb = [R_TMP.alloc([128, 1024], BF16) for _ in range(3)]
            osb = [R_TMP.alloc([128, 512], F32) for _ in range(2)]
            an = [R_TMP.alloc([128, 512], F32) for _ in range(2)]
            asq = [R_TMP.alloc([128, 512], BF16) for _ in range(2)]
            dn = R_TMP.alloc([128, 512], F32)
            P.op("dve", lambda e: e.memset(dn, 1.0), writes=["dn"])
            QKALL = [("qk", fc, t) for fc in range(5) for t in range(4)]
            VALL = [("V0", t) for t in range(4)] + [("V1", t) for t in range(4)] + ["V0c", "V1c"]
            S2 = [ps_all[:, 0:1024], ps_all[:, 1024:2048]]
            S2K = [[PSK[0], PSK[1]], [PSK[2], PSK[3]]]
            OA, oak = psb[4], PSK[4]
            OB, obk = psb[5], PSK[5]
            RBp, rbk = psb[6], PSK[6]
            NA, nak = psb[7], PSK[7]
            steps = [(qc, j, stl) for qc in range(4) for j in range(4) for stl in range(16)]
            nst = len(steps)
            pending = []

            def emit_qk(si):
                qc, j, stl = steps[si]
                q0 = qc * 512
                sl = si % 2
                for u in range(2):
                    rows = slice(0, 64) if u == 0 else slice(64, 128)
                    P.op("pe", lambda e, u=u, rows=rows: e.matmul(
                        S2[sl][:, u * 512:(u + 1) * 512], lhsT=kT[rows, stl * 128:(stl + 1) * 128],
                        rhs=qT[rows, j, q0:q0 + 512], start=True, stop=True),
                        reads=QKALL, writes=[S2K[sl][u]])
                pb = Pb[si % 3]
                P.op("act", lambda e: e.activation(out=pb, in_=S2[sl], func=AF.Exp, scale=0.125),
                     reads=S2K[sl], writes=["Pb%d" % (si % 3)])

            def pair_split(qc, j):
                P.op("dve", lambda e: e.tensor_copy(out=Rhl[0:65, 0, :], in_=Rrow[0:65, :]),
                     reads=["Rrow"], writes=["Rhl"])
                P.op("dve", lambda e: e.tensor_tensor(out=Rhl[0:65, 1, :], in0=Rrow[0:65, :], in1=Rhl[0:65, 0, :],
                                                      op=ALU.subtract),
                     reads=["Rrow", "Rhl"], writes=["Rhl"])

            def pair_bcast(qc, j):
                P.op("pe", lambda e: e.matmul(RBp, lhsT=sel_bf[0:65, :], rhs=Rhl[0:65, 0, :], start=True, stop=False),
                     reads=["cm", "Rhl"], writes=[rbk])
                P.op("pe", lambda e: e.matmul(RBp, lhsT=sel_bf[0:65, :], rhs=Rhl[0:65, 1, :], start=False, stop=True),
                     reads=["cm", "Rhl"], writes=[rbk])
                b2 = j % 2
                q0 = qc * 512
                P.op("dve", lambda e: e.tensor_tensor(out=an[b2][0:64, :], in0=osb[0][0:64, :],
                                                      in1=RBp[0:64, :], op=ALU.mult),
                     reads=[rbk, "osb0"], writes=["an%d" % b2])
                P.op("dve", lambda e: e.tensor_tensor(out=an[b2][64:128, :], in0=osb[1][64:128, :],
                                                      in1=RBp[64:128, :], op=ALU.mult),
                     reads=[rbk, "osb1", "an%d" % b2], writes=["an%d" % b2])
                P.op("dve", lambda e: e.tensor_tensor(out=asq[b2], in0=an[b2], in1=an[b2], op=ALU.mult),
                     reads=["an%d" % b2], writes=["asq%d" % b2])
                P.op("dve", lambda e: e.tensor_scalar(out=attnT[:, j, q0:q0 + 512], in0=an[b2],
                                                      scalar1=col(PR, C_GATT + j), scalar2=None, op0=ALU.mult),
                     reads=["an%d" % b2, "PR"], writes=[("attn", j, qc)])

            def pair_norm(qc, j):
                b2 = j % 2
                q0 = qc * 512
                P.op("pe", lambda e: e.matmul(NA, lhsT=ones_bf, rhs=asq[b2], start=(j == 0), stop=(j == 3)),
                     reads=["cm", "asq%d" % b2], writes=[nak])
                if j == 3:
                    P.op("dve", lambda e: e.tensor_copy(out=r_a[:, q0:q0 + 512], in_=NA),
                         reads=[nak], writes=[("r_a", qc)])
                    pending.append((cur_step["si"] + 5, lambda: qc_rstd(qc)))

            def qc_rstd(qc):
                q0 = qc * 512
                P.op("act", lambda e: e.activation(out=r_a[:, q0:q0 + 512], in_=r_a[:, q0:q0 + 512], func=AF.Ln,
                                                   bias=col(PR, C_EPS), scale=1.0 / 512),
                     reads=[("r_a", qc), "PR"], writes=[("r_a", qc)])
                P.op("act", lambda e: e.activation(out=r_a[:, q0:q0 + 512], in_=r_a[:, q0:q0 + 512], func=AF.Exp,
                                                   scale=-0.5),
                     reads=[("r_a", qc)], writes=[("r_a", qc)])
                pending.append((cur_step["si"] + 3, lambda: qc_scale(qc)))

            def qc_scale(qc):
                q0 = qc * 512
                for j in range(4):
                    P.op("dve", lambda e, j=j: e.tensor_tensor(
                        out=attnT[:, j, q0:q0 + 512], in0=attnT[:, j, q0:q0 + 512], in1=r_a[:, q0:q0 + 512],
                        op=ALU.mult),
                        reads=[("r_a", qc), ("attn", j, qc)], writes=[("attn", j, qc)])

            def emit_pv(si):
                qc, j, stl = steps[si]
                pb = Pb[si % 3]
                P.op("pe", lambda e: e.matmul(OA[0:65, :], lhsT=V0[:, stl, :], rhs=pb[:, 0:512],
                                              start=(stl == 0), stop=(stl == 15)),
                     reads=VALL + ["Pb%d" % (si % 3)], writes=[oak])
                P.op("pe", lambda e: e.matmul(OB, lhsT=V1[:, stl, :], rhs=pb[:, 512:1024],
                                              start=(stl == 0), stop=(stl == 15)),
                     reads=VALL + ["Pb%d" % (si % 3)], writes=[obk])
                if stl == 15:
                    P.op("dve", lambda e: e.tensor_copy(out=osb[0][0:65, :], in_=OA[0:65, :]),
                         reads=[oak], writes=["osb0"])
                    P.op("dve", lambda e: e.tensor_copy(out=osb[1], in_=OB),
                         reads=[obk], writes=["osb1"])
                    P.op("dve", lambda e: e.tensor_copy(out=dn[64:65, :], in_=osb[0][64:65, :]),
                         reads=["osb0"], writes=["dn"])
                    P.op("dve", lambda e: e.tensor_copy(out=dn[0:1, :], in_=osb[1][0:1, :]),
                         reads=["osb1", "dn"], writes=["dn"])
                    P.op("dve", lambda e: e.reciprocal(out=Rrow[0:65, :], in_=dn[0:65, :]),
                         reads=["dn"], writes=["Rrow"])
                    pending.append((si + 6, lambda: pair_split(qc, j)))
                    pending.append((si + 11, lambda: pair_bcast(qc, j)))
                    pending.append((si + 14, lambda: pair_norm(qc, j)))

            cur_step = {"si": 0}

            def conv_unit(cc, tc):
                t0c = tc * 512
                xk_r = [("xl", cc, t) for t in range(4)] + ["xlpad", "PR"]
                dk = ("xc", cc, tc)
                P.op("pool", lambda e: e.tensor_scalar(
                    out=xc[:, cc, t0c:t0c + 512], in0=xlp[:, cc, t0c:t0c + 512],
                    scalar1=col(PR, C_CW + cc * 4 + 0), scalar2=col(PR, C_CB + cc),
                    op0=ALU.mult, op1=ALU.add),
                    reads=xk_r, writes=[dk])
                for jj in range(1, 4):
                    P.op("dve", lambda e, jj=jj: e.scalar_tensor_tensor(
                        out=xc[:, cc, t0c:t0c + 512], in0=xlp[:, cc, t0c + jj:t0c + jj + 512],
                        scalar=col(PR, C_CW + cc * 4 + jj), in1=xc[:, cc, t0c:t0c + 512],
                        op0=ALU.mult, op1=ALU.add), reads=xk_r + [dk], writes=[dk])

            def flush(si):
                cur_step["si"] = min(si, nst)
                progressed = True
                while progressed:
                    progressed = False
                    items = list(pending)
                    pending[:] = []
                    for due, fn in items:
                        if due <= si:
                            fn()
                            progressed = True
                        else:
                            pending.append((due, fn))

            emit_qk(0)
            emit_qk(1)
            for si in range(nst):
                emit_pv(si) if False else None
                if si + 2 < nst:
                    pass
                if si + 2 < nst:
                    emit_qk(si + 2)
                emit_pv(si)
                flush(si)
                if si % 16 == 3:
                    ku = si // 16
                    conv_unit(ku % 4, ku // 4)
            flush(10 ** 9)

            R_D1 = Region(arena, 59392, 86528)
            R_D2 = Region(arena, 107008, ARENA_BYTES)
            R_DX = Region(arena, 86528, 107008)
            w_out_bf = R_DX.alloc([128, 8, D], BF16)
            wgb = [R_DX.alloc([128, 8, 128], BF16), None]
            wub = [R_DX.alloc([128, 8, 128], BF16), None]
            hT = R_D2.alloc([128, 8, 1024], F32)
            ffT = R_D2.alloc([128, NFC, 1024], BF16)
            wdb = [R_D2.alloc([128, NFC, 128], BF16) for _ in range(2)]
            rf = R_D2.alloc([128, 512], F32)
            tgb = [R_D2.alloc([128, 512], F32) for _ in range(2)]
            sgb = [R_D2.alloc([128, 512], F32) for _ in range(2)]
            u2T = R_D1.alloc([128, 8, 1024], BF16)
            wgb[1] = R_D1.alloc([128, 8, 128], BF16)
            wub[1] = R_D1.alloc([128, 8, 128], BF16)
            otile = [R_D1.alloc([128, 512], F32) for _ in range(2)]
            hsq = [R_D1.alloc([128, 512], BF16) for _ in range(2)]
            w_out3 = w_out.rearrange("(k p) n -> p k n", p=128)
            w_gate3 = w_gate.rearrange("(k p) n -> p k n", p=128)
            w_up3 = w_up.rearrange("(k p) n -> p k n", p=128)
            w_down3 = w_down.rearrange("(f p) n -> p f n", p=128)
            XLKEYS = [("xl", cc_, t_) for cc_ in range(4) for t_ in range(4)] + ["xlpad"]

            def prefetch_D(part):
                if part == 0:
                    P.op("pool", lambda e: e.dma_start(out=w_out_bf[:, 0:4, :], in_=w_out3[:, 0:4, :]),
                         writes=["w_outa"] + XLKEYS, dma="w_out0")
                elif part == 1:
                    P.op("pool", lambda e: e.dma_start(out=w_out_bf[:, 4:8, :], in_=w_out3[:, 4:8, :]),
                         writes=["w_outb"] + XLKEYS, dma="w_out1")
                elif part == 2:
                    P.op("pool", lambda e: e.dma_start(out=wgb[0], in_=w_gate3[:, :, 0:128]),
                         writes=["wgb0"] + XLKEYS, dma="wg0")
                elif part == 3:
                    P.op("pool", lambda e: e.dma_start(out=wub[0], in_=w_up3[:, :, 0:128]),
                         writes=["wub0"] + XLKEYS, dma="wu0")

            P.barrier(col(scr, 2))
            R_TMP.reset()
            R_QKV.reset()
            RC = R_QKV
            RL = Region(arena, R_OUTS.start + 2 * 16384 + 8192, R_OUTS.start + 2 * 16384 + 16384)
            xc = R_TMP.alloc([128, 4, S], F32)
            yst = R_TMP.alloc([128, 4, S], F32)
            xcb = [[R_TMP.alloc([128, 512], BF16) for _ in range(2)] for _ in range(2)]
            lsq = [R_TMP.alloc([128, 512], BF16) for _ in range(2)]
            carry = R_TMP.alloc([128, 8], F32)
            RA2 = Region(arena, R_OUTS.start + 2 * 16384, R_OUTS.start + 2 * 16384 + 8192)
            lrub = [[RA2.alloc([128, 512], F32) for _ in range(2)] for _ in range(2)]
            cpend = []
            trb = [RL.alloc([128, 512], F32) for _ in range(2)]
            rlt = [RL.alloc([128, 512], F32) for _ in range(2)]
            ab = [[RC.alloc([128, 512], F32) for _ in range(2)] for _ in range(2)]
            a2b = [[RC.alloc([128, 512], F32) for _ in range(2)] for _ in range(2)]
            tib = [[RC.alloc([128, 512], F32) for _ in range(2)] for _ in range(2)]
            NLB = [(psb[6], PSK[6]), (psb[7], PSK[7])]
            gctr = {"i": 0}

            def stage1(sm, tc, ccs):
                t0 = tc * 512
                for cc in ccs:
                    h = cc % 2
                    P.op("act", lambda e, cc=cc, h=h: e.activation(
                        out=xcb[sm][h], in_=xc[:, cc, t0:t0 + 512], func=AF.Copy),
                        reads=[("xc", cc, tc)], writes=["xcb%d%d" % (sm, h)])
                for cc in ccs:
                    h = cc % 2
                    i = gctr["i"]; gctr["i"] += 1
                    GR, grk = psb[(2 * i) % 6], PSK[(2 * i) % 6]
                    GI, gik = psb[(2 * i + 1) % 6], PSK[(2 * i + 1) % 6]
                    kx = "xcb%d%d" % (sm, h)
                    P.op("pe", lambda e, GR=GR, cc=cc, h=h: e.matmul(
                        GR, lhsT=Wg[:, sm * 8 + cc, :], rhs=xcb[sm][h], start=True, stop=True),
                        reads=["Wg", kx], writes=[grk])
                    P.op("pe", lambda e, GI=GI, cc=cc, h=h: e.matmul(
                        GI, lhsT=Wg[:, sm * 8 + 4 + cc, :], rhs=xcb[sm][h], start=True, stop=True),
                        reads=["Wg", kx], writes=[gik])
                    P.op("act", lambda e, GR=GR, cc=cc: e.activation(
                        out=trb[sm], in_=GR, func=AF.Tanh, bias=col(DV, V_HBR + sm * 4 + cc), scale=0.5),
                        reads=[grk, "DVb"], writes=["trb%d" % sm])
                    P.op("act", lambda e, GI=GI, cc=cc, h=h: e.activation(
                        out=tib[sm][h], in_=GI, func=AF.Tanh, bias=col(DV, V_HBI + sm * 4 + cc), scale=0.5),
                        reads=[gik, "DVb"], writes=["tib%d%d" % (sm, h)])
                    P.op("act", lambda e, cc=cc, h=h: e.activation(
                        out=ab[sm][h], in_=trb[sm], func=AF.Exp, bias=col(DV, V_C1 + sm * 4 + cc),
                        scale=col(DV, V_C1 + sm * 4 + cc)),
                        reads=["trb%d" % sm, "DVc"], writes=["ab%d%d" % (sm, h)])
                    P.op("pool", lambda e, h=h: e.tensor_tensor(
                        out=a2b[sm][h], in0=ab[sm][h], in1=ab[sm][h], op=ALU.mult),
                        reads=["ab%d%d" % (sm, h)], writes=["a2b%d%d" % (sm, h)])
                    P.op("dve", lambda e, cc=cc, h=h: e.scalar_tensor_tensor(
                        out=tib[sm][h], in0=tib[sm][h], scalar=1.0, in1=xc[:, cc, t0:t0 + 512],
                        op0=ALU.add, op1=ALU.mult),
                        reads=["tib%d%d" % (sm, h), ("xc", cc, tc)], writes=["tib%d%d" % (sm, h)])

            def stage2a(sm, ccs):
                for cc in ccs:
                    h = cc % 2
                    k2 = "a2b%d%d" % (sm, h)
                    P.op("act", lambda e, h=h: e.activation(out=a2b[sm][h], in_=a2b[sm][h], func=AF.Sqrt,
                                                            bias=col(DV, V_Q), scale=-0.25),
                         reads=[k2, "DVq"], writes=[k2])

            def stage2b(sm, tc, ccs, first, combine):
                t0 = tc * 512
                deferred = []
                for cc in ccs:
                    h = cc % 2
                    ka, k2, kt = "ab%d%d" % (sm, h), "a2b%d%d" % (sm, h), "tib%d%d" % (sm, h)
                    kc = "carry%d%d" % (sm, cc)
                    cv = carry[:, sm * 4 + cc:sm * 4 + cc + 1]
                    P.op("dve", lambda e, h=h: e.tensor_tensor(
                        out=tib[sm][h], in0=tib[sm][h], in1=a2b[sm][h], op=ALU.mult),
                        reads=[kt, k2], writes=[kt])
                    init = 0.0 if first else cv
                    rk = [] if first else [kc]
                    dst = a2b[sm][h] if combine else yst[:, cc, t0:t0 + 512]
                    dkey = k2 if combine else ("yst", cc, tc)
                    if sm == 0:
                        P.op("dve", lambda e, h=h, dst=dst, init=init: e.tensor_tensor_scan(
                            out=dst, data0=ab[sm][h], data1=tib[sm][h], initial=init, op0=ALU.mult, op1=ALU.add),
                            reads=[ka, kt, k2] + rk, writes=[dkey])
                        P.op("dve", lambda e, dst=dst, cv=cv: e.tensor_copy(out=cv, in_=dst[:, 511:512]),
                             reads=[dkey], writes=[kc])
                    else:
                        P.op("dve", lambda e, h=h, dst=dst, init=init: e.tensor_tensor_scan(
                            out=dst[:, ::-1], data0=ab[sm][h][:, ::-1], data1=tib[sm][h][:, ::-1], initial=init,
                            op0=ALU.mult, op1=ALU.add),
                            reads=[ka, kt, k2] + rk, writes=[dkey])
                        P.op("dve", lambda e, dst=dst, cv=cv: e.tensor_copy(out=cv, in_=dst[:, 0:1]),
                             reads=[dkey], writes=[kc])
                    if combine:
                        cur = a2b[sm][h]
                        lb = lrub[sm][h]
                        kl = "lrub%d%d" % (sm, h)
                        P.op("pool", lambda e, cc=cc, cur=cur: e.tensor_tensor(
                            out=cur, in0=cur, in1=yst[:, cc, t0:t0 + 512], op=ALU.add),
                            reads=[k2, ("yst", cc, tc)], writes=[k2])
                        P.op("pool", lambda e, cc=cc, cur=cur, lb=lb: e.tensor_tensor(
                            out=lb, in0=cur, in1=gg[:, cc, t0:t0 + 512], op=ALU.mult),
                            reads=[k2, ("gg", cc, tc)], writes=[kl])

                        def tail(cc=cc, lb=lb, kl=kl):
                            NL, nlk = NLB[sm]
                            P.op("act", lambda e: e.activation(out=lsq[sm], in_=lb, func=AF.Square),
                                 reads=[kl], writes=["lsq%d" % sm])
                            P.op("pe", lambda e: e.matmul(NL, lhsT=ones_bf, rhs=lsq[sm],
                                                          start=(cc == 0), stop=(cc == 3)),
                                 reads=["cm", "lsq%d" % sm], writes=[nlk])
                            P.op("act", lambda e: e.activation(
                                out=lruT[:, cc, t0:t0 + 512], in_=lb, func=AF.Copy, scale=col(PR, C_GLRU + cc)),
                                reads=[kl, "PR"], writes=[("lru", cc, tc)])
                        cpend.append(tail)

            def cflush():
                while cpend:
                    cpend.pop(0)()

            def finish_tc(sm, tc):
                t0 = tc * 512
                NL, nlk = NLB[sm]
                rstd_from_psum(NL, nlk, 1.0 / 512, rlt[sm], "rlt%d" % sm)
                for cc in range(4):
                    P.op("dve", lambda e, cc=cc: e.tensor_tensor(
                        out=lruT[:, cc, t0:t0 + 512], in0=lruT[:, cc, t0:t0 + 512], in1=rlt[sm], op=ALU.mult),
                        reads=["rlt%d" % sm, ("lru", cc, tc)], writes=[("lru", cc, tc)])

            for p in range(4):
                tcs = (p, 3 - p)
                combine = p >= 2
                for half in range(2):
                    ccs = (2 * half, 2 * half + 1)
                    hi_ = 2 * p + half
                    if 1 <= hi_ <= 4:
                        prefetch_D(hi_ - 1)
                    stage1(0, tcs[0], ccs)
                    stage1(1, tcs[1], ccs)
                    cflush()
                    stage2a(0, ccs)
                    stage2a(1, ccs)
                    stage2b(0, tcs[0], ccs, p == 0, combine)
                    stage2b(1, tcs[1], ccs, p == 0, combine)
                if combine:
                    cpend.append(lambda t_=tcs[0]: finish_tc(0, t_))
                    cpend.append(lambda t_=tcs[1]: finish_tc(1, t_))
            cflush()

            P.barrier(col(scr, 3))
            ii = {"g": 0, "d": 0, "o": 0, "t": 0}
            for hf in range(2):
                h0 = hf * 1024
                for ch in range(2):
                    c0 = ch * 512
                    s0 = h0 + c0
                    g0 = tb + s0
                    tcq = s0 // 512
                    P.op("sp", lambda e, c0=c0, g0=g0: e.dma_start(out=hT[:, :, c0:c0 + 512],
                                                                   in_=xT3[:, :, g0:g0 + 512]),
                         writes=[("h", ch, k) for k in range(8)], dma="xh%d" % ch)
                    for oc in range(8):
                        PA, pak = psum()
                        for j in range(4):
                            P.op("pe", lambda e, PA=PA, j=j, oc=oc, s0=s0: e.matmul(
                                PA, lhsT=w_out_bf[:, j, oc * 128:(oc + 1) * 128], rhs=attnT[:, j, s0:s0 + 512],
                                start=(j == 0), stop=False),
                                reads=["w_outa", ("attn", j, tcq)], writes=[pak])
                        for cc in range(4):
                            P.op("pe", lambda e, PA=PA, cc=cc, oc=oc, s0=s0: e.matmul(
                                PA, lhsT=w_out_bf[:, 4 + cc, oc * 128:(oc + 1) * 128], rhs=lruT[:, cc, s0:s0 + 512],
                                start=False, stop=(cc == 3)),
                                reads=["w_outb", ("lru", cc, tcq)], writes=[pak])
                        hk = ("h", ch, oc)
                        hv = hT[:, oc, c0:c0 + 512]
                        P.op("dve", lambda e, PA=PA, hv=hv: e.tensor_tensor(out=hv, in0=PA, in1=hv, op=ALU.add),
                             reads=[pak, hk], writes=[hk])
                    NF, nfk = psum()
                    for k in range(8):
                        hs = hsq[k % 2]
                        P.op("act", lambda e, k=k, hs=hs, c0=c0: e.activation(
                            out=hs, in_=hT[:, k, c0:c0 + 512], func=AF.Square),
                            reads=[("h", ch, k)], writes=["hsq%d" % (k % 2)])
                        P.op("pe", lambda e, k=k, hs=hs, NF=NF: e.matmul(NF, lhsT=ones_bf, rhs=hs,
                                                                         start=(k == 0), stop=(k == 7)),
                             reads=["cm", "hsq%d" % (k % 2)], writes=[nfk])
                    rstd_from_psum(NF, nfk, 1.0 / D, rf, "rf")
                    for k in range(8):
                        P.op("dve", lambda e, k=k, c0=c0: e.scalar_tensor_tensor(
                            out=u2T[:, k, c0:c0 + 512], in0=hT[:, k, c0:c0 + 512], scalar=col(PR, C_GFFN + k),
                            in1=rf, op0=ALU.mult, op1=ALU.mult),
                            reads=[("h", ch, k), "rf", "PR"], writes=[("u2", ch, k)])
                for fc in range(NFC):
                    sl = ii["g"] % 2; ii["g"] += 1
                    if not (hf == 0 and fc == 0):
                        P.op("pool", lambda e, fc=fc, sl=sl: e.dma_start(
                            out=wgb[sl], in_=w_gate3[:, :, fc * 128:(fc + 1) * 128]),
                            writes=["wgb%d" % sl], dma="wg%d" % sl)
                        P.op("pool", lambda e, fc=fc, sl=sl: e.dma_start(
                            out=wub[sl], in_=w_up3[:, :, fc * 128:(fc + 1) * 128]),
                            writes=["wub%d" % sl], dma="wu%d" % sl)
                    for ch in range(2):
                        c0 = ch * 512
                        G, gk_ = psum()
                        U, uk_ = psum()
                        u2k = [("u2", ch, k) for k in range(8)]
                        for k in range(8):
                            P.op("pe", lambda e, G=G, k=k, sl=sl, c0=c0: e.matmul(
                                G, lhsT=wgb[sl][:, k, :], rhs=u2T[:, k, c0:c0 + 512], start=(k == 0), stop=(k == 7)),
                                reads=["wgb%d" % sl] + u2k, writes=[gk_])
                        for k in range(8):
                            P.op("pe", lambda e, U=U, k=k, sl=sl, c0=c0: e.matmul(
                                U, lhsT=wub[sl][:, k, :], rhs=u2T[:, k, c0:c0 + 512], start=(k == 0), stop=(k == 7)),
                                reads=["wub%d" % sl] + u2k, writes=[uk_])
                        i = ii["o"]; ii["o"] += 1
                        b2 = i % 2
                        P.op("act", lambda e, G=G, b2=b2: e.activation(out=tgb[b2], in_=G, func=AF.Tanh, scale=0.5),
                             reads=[gk_], writes=["tgb%d" % b2])
                        P.op("dve", lambda e, G=G, b2=b2: e.scalar_tensor_tensor(
                            out=sgb[b2], in0=tgb[b2], scalar=1.0, in1=G, op0=ALU.add, op1=ALU.mult),
                            reads=["tgb%d" % b2, gk_], writes=["sgb%d" % b2])
                        P.op("dve", lambda e, U=U, b2=b2, fc=fc, c0=c0: e.scalar_tensor_tensor(
                            out=ffT[:, fc, c0:c0 + 512], in0=sgb[b2], scalar=0.5, in1=U, op0=ALU.mult, op1=ALU.mult),
                            reads=["sgb%d" % b2, uk_], writes=[("ff", ch, fc)])
                for oc in range(8):
                    sl = ii["d"] % 2; ii["d"] += 1
                    P.op("pool", lambda e, oc=oc, sl=sl: e.dma_start(
                        out=wdb[sl], in_=w_down3[:, :, oc * 128:(oc + 1) * 128]),
                        writes=["wdb%d" % sl], dma="wd%d" % sl)
                    for ch in range(2):
                        c0 = ch * 512
                        g0 = tb + h0 + c0
                        Dn, dk_ = psum()
                        ffk = [("ff", ch, fc) for fc in range(NFC)]
                        for fc in range(NFC):
                            P.op("pe", lambda e, Dn=Dn, fc=fc, sl=sl, c0=c0: e.matmul(
                                Dn, lhsT=wdb[sl][:, fc, :], rhs=ffT[:, fc, c0:c0 + 512],
                                start=(fc == 0), stop=(fc == NFC - 1)),
                                reads=["wdb%d" % sl] + ffk, writes=[dk_])
                        i = ii["o"]; ii["o"] += 1
                        ot = otile[i % 2]
                        P.op("dve", lambda e, Dn=Dn, ot=ot, oc=oc, c0=c0: e.tensor_tensor(
                            out=ot, in0=Dn, in1=hT[:, oc, c0:c0 + 512], op=ALU.add),
                            reads=[dk_, ("h", ch, oc)], writes=["ot%d" % (i % 2)])
                        P.op("sp", lambda e, ot=ot, oc=oc, g0=g0: e.dma_start(
                            out=outT3[:, oc, g0:g0 + 512], in_=ot),
                            reads=["ot%d" % (i % 2)], dma="st%d" % (i % 2))

        P.emit(st)
        build.stats = P.stats
    return nc


def _host_consts():
    rows = S // 64
    row = np.repeat(np.arange(rows, dtype=np.float32), 64)
    colp = np.tile(np.arange(64, dtype=np.float32), rows)
    inv = (np.float32(10000.0) ** (-np.arange(0, 32, 2, dtype=np.float32) / np.float32(32))).astype(np.float32)
    ang_r = row[:, None] * inv[None, :]
    ang_c = colp[:, None] * inv[None, :]
    ang = np.concatenate([ang_r, ang_r, ang_c, ang_c], axis=-1).astype(np.float32)
    cos = np.cos(ang).astype(np.float32).T
    sin = np.sin(ang).astype(np.float32).T
    d = np.arange(64)
    sign = np.where(d % 32 < 16, -1.0, 1.0).astype(np.float32)[:, None]
    sin_s = sin * sign
    rope = np.zeros((128, 2, S), np.float32)
    rope[:, 0, :] = np.concatenate([cos, cos], axis=0)
    rope[:, 1, :] = np.concatenate([sin_s, sin_s], axis=0)
    cm = np.zeros((128, 4, 128), np.float32)
    cm[:, 0, :] = 1.0
    for b in range(2):
        cm[b * 64:(b + 1) * 64, 1, b * 64:(b + 1) * 64] = 1.0 / 64.0
    m = np.arange(128)
    src = np.where(m % 32 < 16, m + 16, m - 16)
    cm[src, 2, m] = 1.0
    cm[64, 3, 0:64] = 1.0
    cm[0, 3, 64:128] = 1.0
    return rope, cm


def kernel(x, norm_mix, w_in, q_norm, k_norm, conv_w, conv_b, w_rgate, b_rgate,
           w_igate, b_igate, lru_lambda, out_norm_attn, out_norm_lru, w_out,
           norm_ffn, w_gate, w_up, w_down):
    f32 = np.float32
    x = np.asarray(x, f32)
    B = x.shape[0]
    assert B == N_CORES * NSEQ
    rope, cm = _host_consts()
    hperm = np.concatenate([np.arange(64) + 64 * h for j in range(4) for h in (j, 4 + j)])
    w_in0 = np.asarray(w_in, f32)[0]
    w_in_p = np.ascontiguousarray(np.concatenate([w_in0[:, hperm], w_in0[:, 512:]], axis=1))
    w_out0 = np.asarray(w_out, f32)[0]
    w_out_p = np.ascontiguousarray(np.concatenate([w_out0[hperm, :], w_out0[512:, :]], axis=0))
    prm = np.zeros((128, 72), f32)
    prm[:, 0:8] = np.asarray(norm_mix, f32)[0].reshape(8, 128).T
    prm[:, 8:16] = np.asarray(norm_ffn, f32)[0].reshape(8, 128).T
    prm[:, 16] = np.tile(np.asarray(q_norm, f32)[0], 2)
    prm[:, 17] = np.tile(np.asarray(k_norm, f32)[0], 2)
    prm[:, 18:22] = np.asarray(out_norm_attn, f32)[0][hperm].reshape(4, 128).T
    prm[:, 22:26] = np.asarray(out_norm_lru, f32)[0].reshape(4, 128).T
    cw = np.asarray(conv_w, f32)[0]
    for cc in range(4):
        for j in range(4):
            prm[:, 26 + cc * 4 + j] = cw[j, cc * 128:(cc + 1) * 128]
    prm[:, 42:46] = np.asarray(conv_b, f32)[0].reshape(4, 128).T
    for dr in range(2):
        prm[:, 46 + dr * 4:50 + dr * 4] = np.asarray(b_rgate, f32)[0, dr].reshape(4, 128).T
        prm[:, 54 + dr * 4:58 + dr * 4] = np.asarray(b_igate, f32)[0, dr].reshape(4, 128).T
        prm[:, 62 + dr * 4:66 + dr * 4] = np.asarray(lru_lambda, f32)[0, dr].reshape(4, 128).T
    prm[:, 70] = EPS
    prm[:, 71] = 1.0
    shared = {
        "w_in": w_in_p, "w_out": w_out_p,
        "w_gate": np.ascontiguousarray(np.asarray(w_gate, f32)[0]),
        "w_up": np.ascontiguousarray(np.asarray(w_up, f32)[0]),
        "w_down": np.ascontiguousarray(np.asarray(w_down, f32)[0]),
        "w_rg": np.ascontiguousarray(np.asarray(w_rgate, f32)[0]),
        "w_ig": np.ascontiguousarray(np.asarray(w_igate, f32)[0]),
        "prm": prm, "cmat": cm, "rope": rope,
    }
    in_maps = []
    for c in range(N_CORES):
        xs = x[c * NSEQ:(c + 1) * NSEQ]
        xTc = np.ascontiguousarray(xs.transpose(2, 0, 1).reshape(D, NT))
        m = dict(shared)
        m["xT"] = xTc
        in_maps.append(m)
    nc = build()
    res = run_bass_kernel_spmd(nc, in_maps, core_ids=list(range(N_CORES)))
    out = np.empty((B, S, D), f32)
    for c in range(N_CORES):
        oT = np.asarray(res.results[c]["outT"], f32)
        out[c * NSEQ:(c + 1) * NSEQ] = oT.reshape(D, NSEQ, S).transpose(1, 2, 0)
    return out
```
